# Optimizing a Trainium2 kernel written in Bass

```python
import jax, jax.numpy as jnp
from jax import lax
import numpy as np

D_MODEL = 1024
BATCH = 8
SEQ = 4096
DEPTH = 4

CHUNK = 64
Q_BLOCK = 128
PLE_DIM = 256
D_FF = 2816
CONV_DIM = 512
CONV_GROUPS = 8
CONV_K = 3
N_HEADS = 8
NOPE_DIM = 128
ROPE_DIM = 64
V_DIM = 128
Q_LORA = 384
KV_LORA = 256
ROPE_THETA = 10000.0
EPS = 1e-6
QK_DIM = NOPE_DIM + ROPE_DIM
ATTN_SCALE = QK_DIM ** -0.5
IN_SPLITS = (CONV_DIM, CONV_DIM, CONV_DIM, Q_LORA, KV_LORA, ROPE_DIM, D_MODEL, D_MODEL)
IN_COLS = sum(IN_SPLITS)

kernel_name = "hybrid_conv_mla_macaron_ple_trunk"


def rmsnorm(x, g):
    xf = x.astype(jnp.float32)
    y = xf * lax.rsqrt(jnp.mean(xf * xf, axis=-1, keepdims=True) + EPS)
    return (y * g.astype(jnp.float32)).astype(x.dtype)


def swiglu(x, w_gu, w_down):
    g, u = jnp.split(x @ w_gu, 2, axis=-1)
    return (jax.nn.silu(g) * u) @ w_down


def rope_tables(positions):
    inv_freq = ROPE_THETA ** (-jnp.arange(0, ROPE_DIM, 2, dtype=jnp.float32) / ROPE_DIM)
    ang = positions.astype(jnp.float32)[..., None] * inv_freq
    return jnp.cos(ang), jnp.sin(ang)


def apply_rope(x, cos, sin):
    half = ROPE_DIM // 2
    xf = x.astype(jnp.float32)
    x1, x2 = xf[..., :half], xf[..., half:]
    return jnp.concatenate([x1 * cos - x2 * sin, x2 * cos + x1 * sin], axis=-1).astype(x.dtype)


def short_conv_branch(b_gate, c_gate, v, conv_w, w_conv_out):
    seq = v.shape[1]
    z = c_gate * v
    zp = jnp.pad(z, ((0, 0), (CONV_K - 1, 0), (0, 0)))
    y = conv_w[0] * zp[:, 0:seq]
    for j in range(1, CONV_K):
        y = y + conv_w[j] * zp[:, j:j + seq]
    return (b_gate * y) @ w_conv_out


def block_causal_attention(q_nope, q_rope, k_nope, k_rope, v):
    seq = q_nope.shape[1]
    outs = []
    for i in range(seq // Q_BLOCK):
        q0, q1 = i * Q_BLOCK, (i + 1) * Q_BLOCK
        kn, kr, vb = k_nope[:, :q1], k_rope[:, :q1], v[:, :q1]
        s = (jnp.einsum('bqhd,bkhd->bhqk', q_nope[:, q0:q1], kn)
             + jnp.einsum('bqhd,bkd->bhqk', q_rope[:, q0:q1], kr)).astype(jnp.float32) * ATTN_SCALE
        q_chunk = (q0 + jnp.arange(Q_BLOCK)) // CHUNK
        k_chunk = jnp.arange(q1) // CHUNK
        mask = k_chunk[None, :] <= q_chunk[:, None]
        s = jnp.where(mask, s, -1e30)
        pr = jax.nn.softmax(s, axis=-1).astype(v.dtype)
        outs.append(jnp.einsum('bhqk,bkhd->bqhd', pr, vb))
    return jnp.concatenate(outs, axis=1)


def mla_branch(q_c, kv_c, k_r, q_norm_g, kv_norm_g, w_uq, w_ukv, w_mla_out, cos, sin):
    b, s, _ = q_c.shape
    q = (rmsnorm(q_c, q_norm_g) @ w_uq).reshape(b, s, N_HEADS, QK_DIM)
    q_nope = q[..., :NOPE_DIM]
    q_rope = apply_rope(q[..., NOPE_DIM:], cos[:, :, None, :], sin[:, :, None, :])
    kv = (rmsnorm(kv_c, kv_norm_g) @ w_ukv).reshape(b, s, N_HEADS, NOPE_DIM + V_DIM)
    k_nope, v = kv[..., :NOPE_DIM], kv[..., NOPE_DIM:]
    k_rope = apply_rope(k_r, cos, sin)
    o = block_causal_attention(q_nope, q_rope, k_nope, k_rope, v)
    return o.reshape(b, s, N_HEADS * V_DIM) @ w_mla_out


def setup_inputs(seed: int = 0) -> dict:
    key = jax.random.key(seed)
    ks = jax.random.split(key, 24)
    f32 = jnp.float32

    def w(k, shape, fan_in):
        return jax.random.normal(k, shape, f32) * (fan_in ** -0.5)

    def gain(k, shape):
        return 1.0 + 0.05 * jax.random.normal(k, shape, f32)

    x = jax.random.normal(ks[0], (BATCH, SEQ, D_MODEL), f32)
    p = jax.random.normal(ks[1], (DEPTH, BATCH, SEQ, PLE_DIM), f32)
    offset = jax.random.randint(ks[2], (BATCH, 1), 0, 4096, dtype=jnp.int32)
    positions = offset + jnp.arange(SEQ, dtype=jnp.int32)[None, :]
    return {
        "x": x,
        "p": p,
        "positions": positions,
        "ffn1_norm": gain(ks[3], (DEPTH, D_MODEL)),
        "ffn1_w_gu": w(ks[4], (DEPTH, D_MODEL, 2 * D_FF), D_MODEL),
        "ffn1_w_down": w(ks[5], (DEPTH, D_FF, D_MODEL), D_FF),
        "mix_norm": gain(ks[6], (DEPTH, D_MODEL)),
        "w_in": w(ks[7], (DEPTH, D_MODEL, IN_COLS), D_MODEL),
        "conv_w": w(ks[8], (DEPTH, CONV_K, CONV_DIM), CONV_K),
        "w_conv_out": w(ks[9], (DEPTH, CONV_DIM, D_MODEL), CONV_DIM),
        "q_norm": gain(ks[10], (DEPTH, Q_LORA)),
        "kv_norm": gain(ks[11], (DEPTH, KV_LORA)),
        "w_uq": w(ks[12], (DEPTH, Q_LORA, N_HEADS * QK_DIM), Q_LORA),
        "w_ukv": w(ks[13], (DEPTH, KV_LORA, N_HEADS * (NOPE_DIM + V_DIM)), KV_LORA),
        "w_mla_out": w(ks[14], (DEPTH, N_HEADS * V_DIM, D_MODEL), N_HEADS * V_DIM),
        "w_o": w(ks[15], (DEPTH, D_MODEL, D_MODEL), D_MODEL),
        "ffn2_norm": gain(ks[16], (DEPTH, D_MODEL)),
        "ffn2_w_gu": w(ks[17], (DEPTH, D_MODEL, 2 * D_FF), D_MODEL),
        "ffn2_w_down": w(ks[18], (DEPTH, D_FF, D_MODEL), D_FF),
        "ple_norm": gain(ks[19], (DEPTH, D_MODEL)),
        "w_ple_gate": w(ks[20], (DEPTH, D_MODEL, D_MODEL), D_MODEL),
        "w_ple_proj": w(ks[21], (DEPTH, PLE_DIM, D_MODEL), PLE_DIM),
        "final_norm": gain(ks[22], (D_MODEL,)),
    }


def reference(x, p, positions, ffn1_norm, ffn1_w_gu, ffn1_w_down, mix_norm, w_in, conv_w,
              w_conv_out, q_norm, kv_norm, w_uq, w_ukv, w_mla_out, w_o, ffn2_norm, ffn2_w_gu,
              ffn2_w_down, ple_norm, w_ple_gate, w_ple_proj, final_norm):
    cos, sin = rope_tables(positions)
    split_pts = list(np.cumsum(IN_SPLITS)[:-1])
    h = x
    for i in range(DEPTH):
        h = h + 0.5 * swiglu(rmsnorm(h, ffn1_norm[i]), ffn1_w_gu[i], ffn1_w_down[i])

        u = rmsnorm(h, mix_norm[i])
        b_g, c_g, v_c, q_c, kv_c, k_r, g_conv, g_mla = jnp.split(u @ w_in[i], split_pts, axis=-1)
        y_conv = short_conv_branch(b_g, c_g, v_c, conv_w[i], w_conv_out[i])
        y_mla = mla_branch(q_c, kv_c, k_r, q_norm[i], kv_norm[i], w_uq[i], w_ukv[i],
                           w_mla_out[i], cos, sin)
        merged = jax.nn.sigmoid(g_conv) * y_conv + jax.nn.sigmoid(g_mla) * y_mla
        h = h + merged @ w_o[i]

        h = h + 0.5 * swiglu(rmsnorm(h, ffn2_norm[i]), ffn2_w_gu[i], ffn2_w_down[i])

        gate = jax.nn.sigmoid(rmsnorm(h, ple_norm[i]) @ w_ple_gate[i])
        h = h + gate * (p[i] @ w_ple_proj[i])
    return rmsnorm(h, final_norm)
```

```python
import math
import numpy as np
import concourse.bass as bass
import concourse.mybir as mybir
from concourse.bass_utils import run_bass_kernel_spmd

F32 = mybir.dt.float32
I32 = mybir.dt.int32
BF16 = mybir.dt.bfloat16
AF = mybir.ActivationFunctionType
ALU = mybir.AluOpType

D_MODEL = 1024
DC = 8
D_FF = 2816
FC = 22
CONV_DIM = 512
N_HEADS = 8
NOPE = 128
ROPE = 64
VD = 128
Q_LORA = 384
KV_LORA = 256
PLE = 256
EPS = 1e-6
QK = NOPE + ROPE
ATTN_SCALE = QK ** -0.5
IN_COLS = 4288
OFF_B, OFF_C, OFF_V, OFF_Q, OFF_KV, OFF_KR, OFF_GC, OFF_GM = 0, 512, 1024, 1536, 1920, 2176, 2240, 3264
SUB = 512
MAGIC = 12582912.0
TWO_PI_HI = 6.28125
TWO_PI_LO = 2 * math.pi - 6.28125
PI_SAFE = 3.1415925


class Buf:
    __slots__ = ("name", "last_w", "readers", "aliases")

    def __init__(self, name):
        self.name = name
        self.last_w = None
        self.readers = []
        self.aliases = ()


class Op:
    __slots__ = ("eng", "fn", "dma", "deps", "marked", "cnt", "sem_i", "val", "phase")

    def __init__(self, eng, fn, dma):
        self.eng = eng
        self.fn = fn
        self.dma = dma
        self.deps = ()
        self.marked = False
        self.cnt = 0
        self.sem_i = 0
        self.val = 0
        self.phase = ""


class Prog:
    ENGS = ("pe", "act", "dve", "pool", "sp")
    R = 16

    def __init__(self, nc):
        self.nc = nc
        self.ops = []
        self.engobj = {"pe": nc.tensor, "act": nc.scalar, "dve": nc.vector,
                       "pool": nc.gpsimd, "sp": nc.sync}
        self.csem = {e: nc.alloc_semaphore("c_" + e) for e in self.ENGS}
        self.dsem = {q: [nc.alloc_semaphore(f"d_{q}{i}") for i in range(self.R)]
                     for q in ("sp", "pool")}
        self.dhist = {"sp": [], "pool": []}
        self.phase = ""

    def add(self, eng, fn, reads=(), writes=(), dma=False, extra=()):
        op = Op(eng, fn, dma)
        op.phase = self.phase
        deps = {}
        for d in extra:
            deps[id(d)] = d
        for b in reads:
            w = b.last_w
            if w is not None:
                deps[id(w)] = w
        for b in writes:
            for bb in (b,) + tuple(b.aliases):
                w = bb.last_w
                if w is not None:
                    deps[id(w)] = w
                for r in bb.readers:
                    deps[id(r)] = r
        if dma:
            hist = self.dhist[eng]
            k = len(hist)
            op.sem_i = k % self.R
            op.val = 16 * (k // self.R + 1)
            if k >= self.R:
                d = hist[k - self.R]
                deps[id(d)] = d
            hist.append(op)
        dl = []
        for d in deps.values():
            if (not d.dma) and d.eng == "pe" and eng == "pe" and not dma:
                continue
            dl.append(d)
            if not d.dma:
                d.marked = True
        op.deps = dl
        for b in writes:
            for bb in (b,) + tuple(b.aliases):
                bb.last_w = op
                bb.readers = []
        for b in reads:
            if dma:
                b.readers.append(op)
            else:
                rl = b.readers
                for i, r in enumerate(rl):
                    if (not r.dma) and r.eng == eng:
                        rl[i] = op
                        break
                else:
                    rl.append(op)
        self.ops.append(op)
        return op

    def emit(self):
        cnt = {e: 0 for e in self.ENGS}
        seen = {e: {} for e in self.ENGS}
        for op in self.ops:
            E = self.engobj[op.eng]
            sn = seen[op.eng]
            need = {}
            for d in op.deps:
                if d.dma:
                    sem, val = self.dsem[d.eng][d.sem_i], d.val
                else:
                    sem, val = self.csem[d.eng], d.cnt
                if sn.get(sem.num, 0) >= val:
                    continue
                if need.get(sem.num, (None, 0))[1] < val:
                    need[sem.num] = (sem, val)
            for num, (sem, val) in need.items():
                sn[num] = val
                E.wait_ge(sem, val)
            ins = op.fn()
            if op.dma:
                ins.then_inc(self.dsem[op.eng][op.sem_i], 16)
            elif op.marked:
                cnt[op.eng] += 1
                op.cnt = cnt[op.eng]
                ins.then_inc(self.csem[op.eng], 1)

    def finish(self):
        deps = []
        for q in ("sp", "pool"):
            deps += self.dhist[q][-self.R:]
        op = Op("sp", lambda: self.nc.sync.nop(), False)
        op.deps = deps
        self.ops.append(op)


def build_program(S, depth, T=1024, n_wslots=3, dbg=None):
    assert S % T == 0 and T % SUB == 0
    NS = T // SUB
    NT = S // T
    KCH = 1024
    nc = bass.Bass("TRN2", target_bir_lowering=False)
    pg = Prog(nc)
    PE, ACT, DVE, POOL, SP = "pe", "act", "dve", "pool", "sp"

    def din(name, shape, dt=F32):
        return nc.dram_tensor(name, list(shape), dt, kind="ExternalInput").ap()

    xT = din("xT", [D_MODEL, S])
    pT = din("pT", [depth, PLE, S])
    pos64 = din("pos64", [64, S], I32)
    small = din("small", [128, depth * 32 + 8 + depth * 3 + depth * 2 + depth * 12 + 1])
    w_gu = [din("ffn1_w_gu", [depth, D_MODEL, 2 * D_FF]), din("ffn2_w_gu", [depth, D_MODEL, 2 * D_FF])]
    w_dn = [din("ffn1_w_down", [depth, D_FF, D_MODEL]), din("ffn2_w_down", [depth, D_FF, D_MODEL])]
    w_in = din("w_in", [depth, D_MODEL, IN_COLS])
    w_co = din("w_conv_out", [depth, CONV_DIM, D_MODEL])
    w_uq = din("w_uq", [depth, Q_LORA, N_HEADS * QK])
    w_ukv = din("w_ukv", [depth, KV_LORA, N_HEADS * (NOPE + VD)])
    w_mo = din("w_mla_out", [depth, N_HEADS * VD, D_MODEL])
    w_o = din("w_o", [depth, D_MODEL, D_MODEL])
    w_pg = din("w_ple_gate", [depth, D_MODEL, D_MODEL])
    w_pp = din("w_ple_proj", [depth, PLE, D_MODEL])
    outT = nc.dram_tensor("outT", [D_MODEL, S], F32, kind="ExternalOutput").ap()

    hT_d = nc.dram_tensor("hT_scr", [D_MODEL, S], F32).ap()
    kT_d = nc.dram_tensor("kT_scr", [N_HEADS, 128, S], BF16).ap()
    kr_d = nc.dram_tensor("kr_scr", [64, S], BF16).ap()
    v_d = nc.dram_tensor("v_scr", [N_HEADS, 128, S // 128, 128], BF16).ap()
    rq_d = nc.dram_tensor("rq_scr", [depth, 128, 3, N_HEADS, 64], BF16).ap()
    rk_d = nc.dram_tensor("rk_scr", [depth, 128, DC, 64], BF16).ap()
    rq_b = [Buf(f"rqd{l}") for l in range(depth)]
    rk_b = [Buf(f"rkd{l}") for l in range(depth)]
    cos_d = nc.dram_tensor("cos_scr", [64, S], F32).ap()
    sin_d = nc.dram_tensor("sin_scr", [64, S], F32).ap()

    hT_b = [[Buf(f"hTd{c}_{t}") for t in range(NT)] for c in range(DC)]
    kT_b = [[Buf(f"kTd{h}_{t}") for t in range(NT)] for h in range(N_HEADS)]
    kr_b = [Buf(f"krd{t}") for t in range(NT)]
    v_b = [[Buf(f"vd{h}_{t}") for t in range(NT)] for h in range(N_HEADS)]
    cs_b = [Buf(f"csd{t}") for t in range(NT)]

    def sb(name, shape, dt):
        return nc.alloc_sbuf_tensor("sb_" + name, list(shape), dt)

    ncol_small = small.shape[1]
    small_sb = sb("small", [128, ncol_small], F32)
    small_b = Buf("small")
    o_gain = 0
    o_final = depth * 32
    o_qn = o_final + 8
    o_kvn = o_qn + depth * 3
    o_cw = o_kvn + depth * 2
    o_invf = o_cw + depth * 12

    ones1k = sb("ones1k", [128, 128], BF16)
    ones256 = sb("ones256", [128, 128], BF16)
    ones1 = sb("ones1", [128, 128], BF16)
    const_b = Buf("consts")

    h_sb = sb("h", [128, DC, T], F32)
    h_b = [[Buf(f"h{c}_{s}") for s in range(NS)] for c in range(DC)]
    xn_sb = sb("xn", [128, DC, T], BF16)
    xn_b = [[Buf(f"xn{c}_{s}") for s in range(NS)] for c in range(DC)]

    hid_sb = sb("hid", [128, FC, T], BF16)
    hid_b = [[Buf(f"hid{j}_{s}") for s in range(NS)] for j in range(FC)]
    hid_flat = hid_sb[:].rearrange("p a b -> p (a b)")

    def hid_view(off_chunks, n):
        return hid_flat[:, off_chunks * T:(off_chunks + n) * T]

    qnT_v = [hid_view(hd, 1) for hd in range(N_HEADS)]
    qrT_v = [hid_view(8 + hd, 1) for hd in range(N_HEADS)]
    qn_v = [hid_view(16 + c, 1) for c in range(3)]
    kvn_v = [hid_view(19 + c, 1) for c in range(2)]
    qnT_b = [[Buf(f"qnT{h}_{s}") for s in range(NS)] for h in range(N_HEADS)]
    qrT_b = [[Buf(f"qrT{h}_{s}") for s in range(NS)] for h in range(N_HEADS)]
    qn_b = [[Buf(f"qn{c}_{s}") for s in range(NS)] for c in range(3)]
    kvn_b = [[Buf(f"kvn{c}_{s}") for s in range(NS)] for c in range(2)]
    mix_bufs = [b for l in (qnT_b, qrT_b, qn_b, kvn_b) for row in l for b in row]
    hid_bufs = [b for row in hid_b for b in row]
    for b in mix_bufs:
        b.aliases = tuple(hid_bufs)
    for b in hid_bufs:
        b.aliases = tuple(mix_bufs)

    r_sb = sb("r", [128, 4, T], BF16)
    r_b = [[Buf(f"r{c}_{s}") for s in range(NS)] for c in range(4)]
    o_sb = sb("osb", [128, N_HEADS, T], BF16)
    o_b = [[Buf(f"o{h}_{s}") for s in range(NS)] for h in range(N_HEADS)]

    NKV = 3
    kc_sb = [sb(f"kc{i}", [128, KCH], BF16) for i in range(NKV)]
    kc_b = [Buf(f"kc{i}") for i in range(NKV)]
    vc_sb = [sb(f"vc{i}", [128, KCH // 128, 128], BF16) for i in range(NKV)]
    vc_b = [Buf(f"vc{i}") for i in range(NKV)]
    kr_sb = sb("krall", [128, S], BF16)
    kr_sb_b = Buf("krall")
    mc_elems = max(DC * SUB, 4 * T)
    mc_sb = sb("mergecs", [128, mc_elems], BF16)
    merged_sb = mc_sb[:, 0:DC * SUB].rearrange("p (c n) -> p c n", n=SUB)
    merged_b = [Buf(f"mg{c}") for c in range(DC)]
    cs_sb = mc_sb.bitcast(F32)[0:64, 0:2 * T].rearrange("p (a t) -> p a t", a=2)
    cs_sb_b = Buf("cossin")
    cs_sb_b.aliases = tuple(merged_b)
    for b_ in merged_b:
        b_.aliases = (cs_sb_b,)

    WSLOT = 3584
    w_sb = [sb(f"w{i}", [128, WSLOT], BF16) for i in range(n_wslots)]
    w_b = [[Buf(f"w{i}_{k}") for k in range(4)] for i in range(n_wslots)]
    waux_sb = sb("waux", [128, 2, 1024], BF16)
    waux_b = Buf("waux")
    kp_sb = sb("kstp", [128, 2, T], BF16)
    kst_sb = [kp_sb[:, i, :] for i in range(2)]
    kst_b = [Buf(f"kst{i}") for i in range(2)]
    pbuf_sb = kp_sb
    pbuf_b = Buf("pbuf")
    pbuf_b.aliases = tuple(kst_b)
    for b_ in kst_b:
        b_.aliases = (pbuf_b,)

    NPR = 3
    p_sb = [sb(f"P{i}", [128, SUB], BF16) for i in range(NPR)]
    p_b = [Buf(f"P{i}") for i in range(NPR)]
    NPD = 2
    pd_sb = [sb(f"Pd{i}", [128, SUB], BF16) for i in range(NPD)]
    pd_b = [Buf(f"Pd{i}") for i in range(NPD)]

    NTF = 5
    tf_sb = [sb(f"tf{i}", [128, SUB], F32) for i in range(NTF)]
    tf_b = [Buf(f"tf{i}") for i in range(NTF)]
    NTB = 2
    tb_sb = [sb(f"tb{i}", [128, SUB], BF16) for i in range(NTB)]
    tb_b = [Buf(f"tb{i}") for i in range(NTB)]
    NZ = 1
    zbuf_sb = [sb(f"z{i}", [128, 2 + T], F32) for i in range(NZ)]
    zbuf_b = [Buf(f"z{i}") for i in range(NZ)]
    carry_sb = sb("carry", [128, 4, 2], F32)
    carry_b = [Buf(f"carry{c}") for c in range(4)]
    vst_sb = [sb(f"vst{i}", [128, 1024], BF16) for i in range(2)]
    vst_b = [Buf(f"vst{i}") for i in range(2)]
    krst_sb = hid_view(21, 1)[0:64, :]
    krst_b = Buf("krst")
    krst_b.aliases = tuple(hid_bufs)
    for b_ in hid_bufs:
        b_.aliases = b_.aliases + (krst_b,)

    ost_sb = [sb(f"ost{i}", [128, SUB], F32) for i in range(2)]
    ost_b = [Buf(f"ost{i}") for i in range(2)]
    posi_sb = sb("posi", [64, SUB], I32)
    posi_b = Buf("posi")
    ps = [nc.alloc_psum_tensor(f"ps{i}", [128, SUB], F32) for i in range(8)]
    ps_b = [Buf(f"ps{i}") for i in range(8)]

    print("sbuf bytes remaining per partition:", nc.sbuf_bytes_remaining)

    rot = {"tf": 0, "tb": 0, "w": 0, "ps": 0, "p": 0, "pd": 0, "kv": 0, "z": 0, "kst": 0, "vst": 0, "sb": 0, "ost": 0}

    def nxt(key, n):
        i = rot[key]
        rot[key] = (i + 1) % n
        return i

    def tmp_f():
        i = nxt("tf", NTF)
        return tf_sb[i], tf_b[i]

    def tmp_b():
        i = nxt("tb", NTB)
        return tb_sb[i], tb_b[i]

    def bank():
        i = nxt("ps", 8)
        return ps[i], ps_b[i]

    wpending = {}

    def wslot():
        i = nxt("w", n_wslots)
        pend = {}
        for b in w_b[i]:
            if b.last_w is not None:
                pend[id(b.last_w)] = b.last_w
            for r in b.readers:
                pend[id(r)] = r
            b.last_w = None
            b.readers = []
        for b in w_b[i]:
            wpending[id(b)] = tuple(pend.values())
        return w_sb[i], w_b[i]

    def mm(out, lhsT, rhs, start, stop, reads, writes):
        pg.add(PE, lambda: nc.tensor.matmul(out, lhsT, rhs, start=start, stop=stop), reads, writes)

    def act(out, in_, func, reads, writes, scale=1.0, bias=0.0):
        pg.add(ACT, lambda: nc.scalar.activation(out=out, in_=in_, func=func, bias=bias, scale=scale),
               reads, writes)

    def tt(eng, out, in0, in1, op, reads, writes):
        E = pg.engobj[eng]
        pg.add(eng, lambda: E.tensor_tensor(out=out, in0=in0, in1=in1, op=op), reads, writes)

    def ts(eng, out, in0, s1, s2, op0, op1, reads, writes):
        E = pg.engobj[eng]
        if op1 is None:
            pg.add(eng, lambda: E.tensor_scalar(out=out, in0=in0, scalar1=s1, scalar2=None, op0=op0),
                   reads, writes)
        else:
            pg.add(eng, lambda: E.tensor_scalar(out=out, in0=in0, scalar1=s1, scalar2=s2, op0=op0, op1=op1),
                   reads, writes)

    def stt(eng, out, in0, scalar, in1, op0, op1, reads, writes):
        E = pg.engobj[eng]
        pg.add(eng, lambda: E.scalar_tensor_tensor(out=out, in0=in0, scalar=scalar, in1=in1, op0=op0, op1=op1),
               reads, writes)

    def cp(eng, out, in_, reads, writes):
        E = pg.engobj[eng]
        if eng == ACT:
            pg.add(eng, lambda: E.copy(out=out, in_=in_), reads, writes)
        else:
            pg.add(eng, lambda: E.tensor_copy(out=out, in_=in_), reads, writes)

    def dma(q, out, in_, reads, writes, extra=()):
        E = pg.engobj[q]
        pg.add(q, lambda: E.dma_start(out=out, in_=in_), reads, writes, dma=True, extra=extra)

    def wload(dst, src, dst_b):
        dma(POOL, dst, src, (), (dst_b,), extra=wpending.get(id(dst_b), ()))

    def wview(w3, l, c0, ncols):
        return w3[l].rearrange("(kc p) n -> p kc n", p=128)[:, :, c0:c0 + ncols]

    def slot_view(slot, kc, ncols, off=0):
        return slot[:, off:off + kc * ncols].rearrange("p (k n) -> p k n", n=ncols)

    def col(off):
        return small_sb[:, off:off + 1]

    def ssl(s):
        return slice(s * SUB, (s + 1) * SUB)

    dma(SP, small_sb[:], small, (), (small_b,))
    pg.add(DVE, lambda: nc.vector.memset(ones1k[:], 1.0 / 1024.0), (), (const_b,))
    pg.add(DVE, lambda: nc.vector.memset(ones256[:], 1.0 / 256.0), (), (const_b,))
    pg.add(DVE, lambda: nc.vector.memset(ones1[:], 1.0), (), (const_b,))
    for i in range(NPD):
        pg.add(DVE, (lambda i=i: nc.vector.memset(pd_sb[i][:], 0.0)), (), (pd_b[i],))
    pg.add(DVE, lambda: nc.vector.memset(kr_sb[64:128, :], 0.0), (), (kr_sb_b,))
    for i in range(n_wslots):
        pg.add(DVE, (lambda i=i: nc.vector.memset(w_sb[i][:], 0.0)), (), tuple(w_b[i]))

    for l in range(depth):
        slot, sbs = wslot()
        src = slot[:, 0:1536].rearrange("p (k h c) -> p k h c", k=3, h=N_HEADS)
        dst = slot[:, 1536:3072].rearrange("p (k h c) -> p k h c", k=3, h=N_HEADS)
        wq = w_uq[l].rearrange("(kc p) (h c) -> p kc h c", p=128, c=QK)
        for kc in range(3):
            wload(src[:, kc], wq[:, kc, :, NOPE:QK], sbs[0])
        ts(DVE, dst[:, :, :, 0:32], src[:, :, :, 32:64], -1.0, None, ALU.mult, None, (sbs[0],), (sbs[1],))
        cp(DVE, dst[:, :, :, 32:64], src[:, :, :, 0:32], (sbs[0], sbs[1]), (sbs[1],))
        dma(SP, rq_d[l], dst, (sbs[1],), (rq_b[l],))
        slot, sbs = wslot()
        src = slot[:, 0:512].rearrange("p (k c) -> p k c", k=DC)
        dst = slot[:, 512:1024].rearrange("p (k c) -> p k c", k=DC)
        wload(src, wview(w_in, l, OFF_KR, 64), sbs[0])
        ts(DVE, dst[:, :, 0:32], src[:, :, 32:64], -1.0, None, ALU.mult, None, (sbs[0],), (sbs[1],))
        cp(DVE, dst[:, :, 32:64], src[:, :, 0:32], (sbs[0], sbs[1]), (sbs[1],))
        dma(SP, rk_d[l], dst, (sbs[1],), (rk_b[l],))

    for t in range(NT):
        tsl = slice(t * T, (t + 1) * T)
        for s in range(NS):
            g0 = t * T + s * SUB
            a_t, a_b = tmp_f()
            k_t, k_b = tmp_f()
            r_t, r_bb = tmp_f()
            c_t, c_b = tmp_f()
            dma(SP, posi_sb[:], pos64[:, g0:g0 + SUB], (), (posi_b,))
            cp(DVE, a_t[0:64, :], posi_sb[:], (posi_b,), (a_b,))
            ts(DVE, a_t[0:64, :], a_t[0:64, :], small_sb[0:64, o_invf:o_invf + 1], None, ALU.mult, None,
               (a_b, small_b), (a_b,))
            ts(DVE, k_t[0:64, :], a_t[0:64, :], 1.0 / (2 * math.pi), MAGIC, ALU.mult, ALU.add, (a_b,), (k_b,))
            ts(DVE, k_t[0:64, :], k_t[0:64, :], MAGIC, None, ALU.subtract, None, (k_b,), (k_b,))
            stt(DVE, r_t[0:64, :], k_t[0:64, :], -TWO_PI_HI, a_t[0:64, :], ALU.mult, ALU.add, (k_b, a_b), (r_bb,))
            stt(DVE, r_t[0:64, :], k_t[0:64, :], -TWO_PI_LO, r_t[0:64, :], ALU.mult, ALU.add, (k_b, r_bb), (r_bb,))
            ts(DVE, c_t[0:64, :], r_t[0:64, :], math.pi / 2, None, ALU.add, None, (r_bb,), (c_b,))
            ts(DVE, k_t[0:64, :], c_t[0:64, :], math.pi, None, ALU.is_gt, None, (c_b,), (k_b,))
            stt(DVE, c_t[0:64, :], k_t[0:64, :], -2 * math.pi, c_t[0:64, :], ALU.mult, ALU.add, (k_b, c_b), (c_b,))
            ts(DVE, r_t[0:64, :], r_t[0:64, :], -PI_SAFE, PI_SAFE, ALU.max, ALU.min, (r_bb,), (r_bb,))
            ts(DVE, c_t[0:64, :], c_t[0:64, :], -PI_SAFE, PI_SAFE, ALU.max, ALU.min, (c_b,), (c_b,))
            act(a_t[0:64, :], r_t[0:64, :], AF.Sin, (r_bb,), (a_b,))
            act(k_t[0:64, :], c_t[0:64, :], AF.Sin, (c_b,), (k_b,))
            dma(SP, sin_d[:, g0:g0 + SUB], a_t[0:64, :], (a_b,), (cs_b[t],))
            dma(SP, cos_d[:, g0:g0 + SUB], k_t[0:64, :], (k_b,), (cs_b[t],))

    def rms_norm_tile(l, which, dst_sb, dst_bufs):
        for s in range(NS):
            pN, pN_b = bank()
            for c in range(DC):
                sq, sq_b = tmp_b()
                act(sq[:], h_sb[:, c, ssl(s)], AF.Square, (h_b[c][s],), (sq_b,))
                mm(pN[:], ones1k[:], sq[:], c == 0, c == DC - 1, (sq_b, const_b), (pN_b,))
            sd, sd_b = tmp_f()
            act(sd[:], pN[:], AF.Sqrt, (pN_b,), (sd_b,), scale=1.0, bias=EPS)
            pg.add(DVE, (lambda sd=sd: nc.vector.reciprocal(out=sd[:], in_=sd[:])), (sd_b,), (sd_b,))
            for c in range(DC):
                if which == "final":
                    gc = col(o_final + c)
                else:
                    gc = col(o_gain + l * 32 + which * 8 + c)
                stt(DVE, dst_sb[:, c, ssl(s)], h_sb[:, c, ssl(s)], gc, sd[:], ALU.mult, ALU.mult,
                    (h_b[c][s], sd_b, small_b), (dst_bufs[c][s],))

    def ffn(l, f):
        pg.phase = "ffn_norm"
        rms_norm_tile(l, 0 if f == 0 else 2, xn_sb, xn_b)
        pg.phase = "ffn_gu"
        for j in range(FC):
            slot, sbs = wslot()
            sv = slot_view(slot, DC, 256)
            wload(sv[:, :, 0:128], wview(w_gu[f], l, j * 128, 128), sbs[0])
            wload(sv[:, :, 128:256], wview(w_gu[f], l, D_FF + j * 128, 128), sbs[1])
            for s in range(NS):
                pgk, pg_b = bank()
                pu, pu_b = bank()
                for kc in range(DC):
                    mm(pgk[:], sv[:, kc, 0:128], xn_sb[:, kc, ssl(s)], kc == 0, kc == DC - 1,
                       (sbs[0], xn_b[kc][s]), (pg_b,))
                for kc in range(DC):
                    mm(pu[:], sv[:, kc, 128:256], xn_sb[:, kc, ssl(s)], kc == 0, kc == DC - 1,
                       (sbs[1], xn_b[kc][s]), (pu_b,))
                sg, sg_b = tmp_f()
                act(sg[:], pgk[:], AF.Silu, (pg_b,), (sg_b,))
                tt(DVE, hid_sb[:, j, ssl(s)], sg[:], pu[:], ALU.mult, (sg_b, pu_b), (hid_b[j][s],))
        pg.phase = "ffn_down"
        for c in range(DC):
            slot, sbs = wslot()
            sv = slot_view(slot, FC, 128)
            wload(sv, wview(w_dn[f], l, c * 128, 128), sbs[0])
            for s in range(NS):
                po, po_b = bank()
                for kc in range(FC):
                    mm(po[:], sv[:, kc, :], hid_sb[:, kc, ssl(s)], kc == 0, kc == FC - 1,
                       (sbs[0], hid_b[kc][s]), (po_b,))
                stt(DVE, h_sb[:, c, ssl(s)], po[:], 0.5, h_sb[:, c, ssl(s)], ALU.mult, ALU.add,
                    (po_b, h_b[c][s]), (h_b[c][s],))

    def small_norm(srcs, src_bs, nch, ones_t, scale, gain_off, dst_views, dst_bufs, s):
        pN, pN_b = bank()
        for c in range(nch):
            sq, sq_b = tmp_b()
            act(sq[:], srcs[c][:], AF.Square, (src_bs[c],), (sq_b,))
            mm(pN[:], ones_t[:], sq[:], c == 0, c == nch - 1, (sq_b, const_b), (pN_b,))
        sd, sd_b = tmp_f()
        act(sd[:], pN[:], AF.Sqrt, (pN_b,), (sd_b,), scale=scale, bias=EPS)
        pg.add(DVE, (lambda sd=sd: nc.vector.reciprocal(out=sd[:], in_=sd[:])), (sd_b,), (sd_b,))
        for c in range(nch):
            stt(DVE, dst_views[c][:, ssl(s)], srcs[c][:], col(gain_off + c), sd[:], ALU.mult, ALU.mult,
                (src_bs[c], sd_b, small_b), (dst_bufs[c][s],))

    def rope_combine(px, px_b, pr, pr_b, s, out_ap, out_b):
        t1, t1_b = tmp_f()
        t2, t2_b = tmp_f()
        tt(DVE, t1[0:64, :], px[0:64, :], cs_sb[:, 0, ssl(s)], ALU.mult, (px_b, cs_sb_b), (t1_b,))
        tt(DVE, t2[0:64, :], pr[0:64, :], cs_sb[:, 1, ssl(s)], ALU.mult, (pr_b, cs_sb_b), (t2_b,))
        tt(DVE, out_ap, t1[0:64, :], t2[0:64, :], ALU.add, (t1_b, t2_b), (out_b,))

    def build_rot(slot3, kc, c_x1, c_x2, c_dst, sb_src, sb_dst):
        ts(DVE, slot3[:, 0:kc, c_dst:c_dst + 32], slot3[:, 0:kc, c_x2:c_x2 + 32], -1.0, None, ALU.mult, None,
           (sb_src,), (sb_dst,))
        cp(DVE, slot3[:, 0:kc, c_dst + 32:c_dst + 64], slot3[:, 0:kc, c_x1:c_x1 + 32], (sb_src, sb_dst), (sb_dst,))

    def mixer_pre(l, t):
        g0 = t * T
        pg.phase = "mix_norm"
        rms_norm_tile(l, 1, xn_sb, xn_b)
        pg.phase = "mix_conv"
        dma(SP, cs_sb[:, 0, :], cos_d[:, g0:g0 + T], (cs_b[t],), (cs_sb_b,))
        dma(SP, cs_sb[:, 1, :], sin_d[:, g0:g0 + T], (cs_b[t],), (cs_sb_b,))
        for c in range(4):
            slot, sbs = wslot()
            sv = slot_view(slot, DC, 384)
            wload(sv[:, :, 0:128], wview(w_in, l, OFF_B + c * 128, 128), sbs[0])
            wload(sv[:, :, 128:256], wview(w_in, l, OFF_C + c * 128, 128), sbs[1])
            wload(sv[:, :, 256:384], wview(w_in, l, OFF_V + c * 128, 128), sbs[2])
            zi = nxt("z", NZ)
            z, z_b = zbuf_sb[zi], zbuf_b[zi]
            if t == 0:
                pg.add(DVE, (lambda z=z: nc.vector.memset(z[:, 0:2], 0.0)), (), (z_b,))
            else:
                cp(DVE, z[:, 0:2], carry_sb[:, c, :], (carry_b[c],), (z_b,))
            for s in range(NS):
                pB, pB_b = bank()
                pC, pC_b = bank()
                pV, pV_b = bank()
                for (pp_, pp_b, o, k) in ((pB, pB_b, 0, 0), (pC, pC_b, 128, 1), (pV, pV_b, 256, 2)):
                    for kc in range(DC):
                        mm(pp_[:], sv[:, kc, o:o + 128], xn_sb[:, kc, ssl(s)], kc == 0, kc == DC - 1,
                           (sbs[k], xn_b[kc][s]), (pp_b,))
                cs_, cs_bb = tmp_f()
                cp(ACT, cs_[:], pC[:], (pC_b,), (cs_bb,))
                zo = 2 + s * SUB
                tt(DVE, z[:, zo:zo + SUB], cs_[:], pV[:], ALU.mult, (cs_bb, pV_b), (z_b,))
                y, y_b = tmp_f()
                cw = o_cw + l * 12
                ts(DVE, y[:], z[:, zo:zo + SUB], col(cw + 2 * 4 + c), None, ALU.mult, None, (z_b, small_b), (y_b,))
                stt(DVE, y[:], z[:, zo - 1:zo - 1 + SUB], col(cw + 1 * 4 + c), y[:], ALU.mult, ALU.add,
                    (z_b, y_b, small_b), (y_b,))
                stt(DVE, y[:], z[:, zo - 2:zo - 2 + SUB], col(cw + 0 * 4 + c), y[:], ALU.mult, ALU.add,
                    (z_b, y_b, small_b), (y_b,))
                tt(DVE, r_sb[:, c, ssl(s)], y[:], pB[:], ALU.mult, (y_b, pB_b), (r_b[c][s],))
            cp(DVE, carry_sb[:, c, :], z[:, T:T + 2], (z_b,), (carry_b[c],))
        pg.phase = "mix_qc"
        slot, sbs = wslot()
        sv = slot_view(slot, DC, 384)
        wload(sv, wview(w_in, l, OFF_Q, 384), sbs[0])
        for s in range(NS):
            srcs, src_bs = [], []
            for c in range(3):
                pq, pq_b = bank()
                for kc in range(DC):
                    mm(pq[:], sv[:, kc, c * 128:(c + 1) * 128], xn_sb[:, kc, ssl(s)], kc == 0, kc == DC - 1,
                       (sbs[0], xn_b[kc][s]), (pq_b,))
                qf, qf_b = tmp_f()
                cp(ACT, qf[:], pq[:], (pq_b,), (qf_b,))
                srcs.append(qf)
                src_bs.append(qf_b)
            small_norm(srcs, src_bs, 3, ones256, 256.0 / 384.0, o_qn + l * 3, qn_v, qn_b, s)
        pg.phase = "mix_kvc"
        slot, sbs = wslot()
        sv = slot_view(slot, DC, 448)
        wload(sv[:, :, 0:320], wview(w_in, l, OFF_KV, 320), sbs[0])
        dma(POOL, sv[:, :, 320:384], rk_d[l], (rk_b[l],), (sbs[1],), extra=wpending.get(id(sbs[1]), ()))
        for s in range(NS):
            srcs, src_bs = [], []
            for c in range(2):
                pk, pk_b = bank()
                for kc in range(DC):
                    mm(pk[:], sv[:, kc, c * 128:(c + 1) * 128], xn_sb[:, kc, ssl(s)], kc == 0, kc == DC - 1,
                       (sbs[0], xn_b[kc][s]), (pk_b,))
                kf, kf_b = tmp_f()
                cp(ACT, kf[:], pk[:], (pk_b,), (kf_b,))
                srcs.append(kf)
                src_bs.append(kf_b)
            small_norm(srcs, src_bs, 2, ones256, 1.0, o_kvn + l * 2, kvn_v, kvn_b, s)
            px, px_b = bank()
            pr, pr_b = bank()
            for kc in range(DC):
                mm(px[:], sv[:, kc, 256:384], xn_sb[:, kc, ssl(s)], kc == 0, kc == DC - 1,
                   (sbs[0], sbs[1], xn_b[kc][s]), (px_b,))
            for kc in range(DC):
                mm(pr[:], sv[:, kc, 320:448], xn_sb[:, kc, ssl(s)], kc == 0, kc == DC - 1,
                   (sbs[1], xn_b[kc][s]), (pr_b,))
            rope_combine(px, px_b, pr, pr_b, s, krst_sb[:, ssl(s)], krst_b)
        dma(SP, kr_d[:, g0:g0 + T], krst_sb, (krst_b,), (kr_b[t],))
        pg.phase = "mix_uq"
        for hd in range(N_HEADS):
            pg.add(ACT, (lambda hd=hd: nc.scalar.memzero(qrT_v[hd][64:128, :])), (), tuple(qrT_b[hd]))
        for hd in range(N_HEADS):
            slot, sbs = wslot()
            sv = slot_view(slot, 3, 320)
            wload(sv[:, :, 0:192], wview(w_uq, l, hd * QK, QK), sbs[0])
            dma(POOL, sv[:, :, 192:256], rq_d[l][:, :, hd, :], (rq_b[l],), (sbs[1],),
                extra=wpending.get(id(sbs[1]), ()))
            for s in range(NS):
                pn, pn_b = bank()
                px, px_b = bank()
                pr, pr_b = bank()
                for kc in range(3):
                    mm(pn[:], sv[:, kc, 0:128], qn_v[kc][:, ssl(s)], kc == 0, kc == 2, (sbs[0], qn_b[kc][s]), (pn_b,))
                for kc in range(3):
                    mm(px[:], sv[:, kc, 128:256], qn_v[kc][:, ssl(s)], kc == 0, kc == 2,
                       (sbs[0], sbs[1], qn_b[kc][s]), (px_b,))
                for kc in range(3):
                    mm(pr[:], sv[:, kc, 192:320], qn_v[kc][:, ssl(s)], kc == 0, kc == 2,
                       (sbs[1], qn_b[kc][s]), (pr_b,))
                cp(ACT, qnT_v[hd][:, ssl(s)], pn[:], (pn_b,), (qnT_b[hd][s],))
                rope_combine(px, px_b, pr, pr_b, s, qrT_v[hd][0:64, ssl(s)], qrT_b[hd][s])
        pg.phase = "mix_uk"
        for hd in range(N_HEADS):
            slot, sbs = wslot()
            sv = slot_view(slot, 2, 128)
            wload(sv, wview(w_ukv, l, hd * 256, 128), sbs[0])
            ki = nxt("kst", 2)
            for s in range(NS):
                pk, pk_b = bank()
                for kc in range(2):
                    mm(pk[:], sv[:, kc, :], kvn_v[kc][:, ssl(s)], kc == 0, kc == 1, (sbs[0], kvn_b[kc][s]), (pk_b,))
                cp(ACT, kst_sb[ki][:, ssl(s)], pk[:], (pk_b,), (kst_b[ki],))
            dma(SP, kT_d[hd, :, g0:g0 + T], kst_sb[ki], (kst_b[ki],), (kT_b[hd][t],))
        pg.phase = "mix_v"
        wv = w_ukv[l].rearrange("(kc p) (h two d) -> p kc h two d", p=128, two=2, d=128)
        for kc in range(2):
            wload(waux_sb[:, kc, :].rearrange("p (h d) -> p h d", d=128), wv[:, kc, :, 1, :], waux_b)
        for blk in range(T // 128):
            s = blk // 4
            vi = nxt("vst", 2)
            bsl = slice(blk * 128, (blk + 1) * 128)
            for half in range(2):
                pv, pv_b = bank()
                for kc in range(2):
                    mm(pv[:], kvn_v[kc][:, bsl], waux_sb[:, kc, half * 512:(half + 1) * 512], kc == 0, kc == 1,
                       (waux_b, kvn_b[kc][s]), (pv_b,))
                if half == 0:
                    cp(ACT, vst_sb[vi][:, 0:512], pv[:], (pv_b,), (vst_b[vi],))
                else:
                    cp(DVE, vst_sb[vi][:, 512:1024], pv[:], (pv_b,), (vst_b[vi],))
            gb = (g0 // 128) + blk
            dma(SP, v_d[:, :, gb, :].rearrange("h p d -> p h d"),
                vst_sb[vi][:].rearrange("p (h d) -> p h d", d=128),
                (vst_b[vi],), tuple(v_b[hd][t] for hd in range(N_HEADS)))

    def attention(l, t):
        pg.phase = "attn"
        g0 = t * T
        nk = g0 + T
        dma(SP, kr_sb[0:64, 0:nk], kr_d[:, 0:nk], tuple(kr_b[:t + 1]), (kr_sb_b,))
        items = []
        for hd in range(N_HEADS):
            for s in range(NS):
                q0 = g0 + s * SUB
                nkb = (q0 + SUB) // 128
                for kb in range(nkb):
                    items.append((hd, s, kb, nkb))
        state = {}

        def emit_scores(it):
            hd, s, kb, nkb = it
            q0 = g0 + s * SUB
            k0 = kb * 128
            if kb % (KCH // 128) == 0:
                ch = k0 // KCH
                ci = nxt("kv", NKV)
                kend = min((ch + 1) * KCH, q0 + SUB)
                klen = kend - ch * KCH
                tl = tuple(range(ch * KCH // T, (kend - 1) // T + 1))
                dma(SP, kc_sb[ci][:, 0:klen], kT_d[hd, :, ch * KCH:kend],
                    tuple(kT_b[hd][tt_] for tt_ in tl), (kc_b[ci],))
                dma(SP, vc_sb[ci][:, 0:klen // 128, :], v_d[hd, :, ch * (KCH // 128):kend // 128, :],
                    tuple(v_b[hd][tt_] for tt_ in tl), (vc_b[ci],))
                state["ci"] = ci
            ci = state["ci"]
            qoff = max(0, k0 - q0)
            N = SUB - qoff
            kl = k0 % KCH
            bi = nxt("sb", 4)
            pS, pS_b = ps[bi], ps_b[bi]
            qsl = slice(s * SUB + qoff, (s + 1) * SUB)
            mm(pS[:, 0:N], kc_sb[ci][:, kl:kl + 128], qnT_v[hd][:, qsl], True, False,
               (kc_b[ci], qnT_b[hd][s]), (pS_b,))
            mm(pS[:, 0:N], kr_sb[:, k0:k0 + 128], qrT_v[hd][:, qsl], False, True,
               (kr_sb_b, qrT_b[hd][s]), (pS_b,))
            return (ci, kl, qoff, N, pS, pS_b)

        def emit_rest(it, sc):
            hd, s, kb, nkb = it
            ci, kl, qoff, N, pS, pS_b = sc
            q0 = g0 + s * SUB
            diag = kb * 128 >= q0
            par = (hd * NS + s) % 2
            pO, pO_b = ps[4 + 2 * par], ps_b[4 + 2 * par]
            pL, pL_b = ps[5 + 2 * par], ps_b[5 + 2 * par]
            if not diag:
                pi = nxt("p", NPR)
                P, P_b = p_sb[pi], p_b[pi]
                act(P[:, 0:N], pS[:, 0:N], AF.Exp, (pS_b,), (P_b,), scale=ATTN_SCALE)
            else:
                pi = nxt("pd", NPD)
                P, P_b = pd_sb[pi], pd_b[pi]
                act(P[0:64, 0:N], pS[0:64, 0:N], AF.Exp, (pS_b,), (P_b,), scale=ATTN_SCALE)
                act(P[64:128, 64:N], pS[64:128, 64:N], AF.Exp, (pS_b,), (P_b,), scale=ATTN_SCALE)
            mm(pO[:, qoff:SUB], vc_sb[ci][:, kl // 128, :], P[:, 0:N], kb == 0, kb == nkb - 1,
               (vc_b[ci], P_b), (pO_b,))
            mm(pL[:, qoff:SUB], ones1[:], P[:, 0:N], kb == 0, kb == nkb - 1, (const_b, P_b), (pL_b,))
            if kb == nkb - 1:
                rl, rl_b = tmp_f()
                pg.add(DVE, (lambda rl=rl, pL=pL: nc.vector.reciprocal(out=rl[:], in_=pL[:])), (pL_b,), (rl_b,))
                tt(DVE, o_sb[:, hd, ssl(s)], pO[:], rl[:], ALU.mult, (pO_b, rl_b), (o_b[hd][s],))

        DEPTH_SW = 2
        scs = []
        for i in range(min(DEPTH_SW, len(items))):
            scs.append(emit_scores(items[i]))
        for i, it in enumerate(items):
            if i + DEPTH_SW < len(items):
                scs.append(emit_scores(items[i + DEPTH_SW]))
            emit_rest(it, scs[i])

    def merge_and_out(l, t):
        for s in range(NS):
            pg.phase = "merge"
            for c in range(DC):
                slot, sbs = wslot()
                sv = slot_view(slot, 28, 128)
                wload(sv[:, 0:8, :], wview(w_in, l, OFF_GC + c * 128, 128), sbs[0])
                wload(sv[:, 8:16, :], wview(w_in, l, OFF_GM + c * 128, 128), sbs[1])
                wload(sv[:, 16:20, :], wview(w_co, l, c * 128, 128), sbs[2])
                wload(sv[:, 20:28, :], wview(w_mo, l, c * 128, 128), sbs[3])
                pgc, pgc_b = bank()
                pyc, pyc_b = bank()
                pgm, pgm_b = bank()
                pym, pym_b = bank()
                for kc in range(DC):
                    mm(pgc[:], sv[:, kc, :], xn_sb[:, kc, ssl(s)], kc == 0, kc == DC - 1, (sbs[0], xn_b[kc][s]), (pgc_b,))
                for kc in range(4):
                    mm(pyc[:], sv[:, 16 + kc, :], r_sb[:, kc, ssl(s)], kc == 0, kc == 3, (sbs[2], r_b[kc][s]), (pyc_b,))
                for kc in range(DC):
                    mm(pgm[:], sv[:, 8 + kc, :], xn_sb[:, kc, ssl(s)], kc == 0, kc == DC - 1, (sbs[1], xn_b[kc][s]), (pgm_b,))
                for kc in range(DC):
                    mm(pym[:], sv[:, 20 + kc, :], o_sb[:, kc, ssl(s)], kc == 0, kc == DC - 1, (sbs[3], o_b[kc][s]), (pym_b,))
                s1, s1_b = tmp_f()
                s2, s2_b = tmp_f()
                act(s1[:], pgc[:], AF.Sigmoid, (pgc_b,), (s1_b,))
                act(s2[:], pgm[:], AF.Sigmoid, (pgm_b,), (s2_b,))
                tt(DVE, s1[:], s1[:], pyc[:], ALU.mult, (s1_b, pyc_b), (s1_b,))
                tt(DVE, s2[:], s2[:], pym[:], ALU.mult, (s2_b, pym_b), (s2_b,))
                tt(DVE, merged_sb[:, c, :], s1[:], s2[:], ALU.add, (s1_b, s2_b), (merged_b[c],))
            pg.phase = "w_o"
            for c in range(DC):
                slot, sbs = wslot()
                sv = slot_view(slot, DC, 128)
                wload(sv, wview(w_o, l, c * 128, 128), sbs[0])
                po, po_b = bank()
                for kc in range(DC):
                    mm(po[:], sv[:, kc, :], merged_sb[:, kc, :], kc == 0, kc == DC - 1, (sbs[0], merged_b[kc]), (po_b,))
                tt(DVE, h_sb[:, c, ssl(s)], po[:], h_sb[:, c, ssl(s)], ALU.add, (po_b, h_b[c][s]), (h_b[c][s],))

    def ple(l, t):
        g0 = t * T
        pg.phase = "ple_norm"
        rms_norm_tile(l, 3, xn_sb, xn_b)
        pg.phase = "ple"
        wload(waux_sb[:], w_pp[l].rearrange("(kc p) n -> p kc n", p=128), waux_b)
        dma(POOL, pbuf_sb[:], pT[l].rearrange("(kc p) s -> p kc s", p=128)[:, :, g0:g0 + T], (), (pbuf_b,))
        for c in range(DC):
            slot, sbs = wslot()
            sv = slot_view(slot, DC, 128)
            wload(sv, wview(w_pg, l, c * 128, 128), sbs[0])
            for s in range(NS):
                pgt, pgt_b = bank()
                ppp, ppp_b = bank()
                for kc in range(DC):
                    mm(pgt[:], sv[:, kc, :], xn_sb[:, kc, ssl(s)], kc == 0, kc == DC - 1, (sbs[0], xn_b[kc][s]), (pgt_b,))
                for kc in range(2):
                    mm(ppp[:], waux_sb[:, kc, c * 128:(c + 1) * 128], pbuf_sb[:, kc, ssl(s)], kc == 0, kc == 1,
                       (waux_b, pbuf_b), (ppp_b,))
                sg, sg_b = tmp_f()
                act(sg[:], pgt[:], AF.Sigmoid, (pgt_b,), (sg_b,))
                tt(DVE, sg[:], sg[:], ppp[:], ALU.mult, (sg_b, ppp_b), (sg_b,))
                tt(DVE, h_sb[:, c, ssl(s)], h_sb[:, c, ssl(s)], sg[:], ALU.add, (sg_b, h_b[c][s]), (h_b[c][s],))

    for l in range(depth):
        for t in range(NT):
            g0 = t * T
            src = xT if l == 0 else hT_d
            for c in range(DC):
                rd = () if l == 0 else (hT_b[c][t],)
                dma(SP, h_sb[:, c, :], src[c * 128:(c + 1) * 128, g0:g0 + T], rd, tuple(h_b[c]))
            def dump():
                for c in range(DC):
                    dma(SP, outT[c * 128:(c + 1) * 128, g0:g0 + T], h_sb[:, c, :], tuple(h_b[c]), ())
            stages = [lambda: ffn(l, 0), lambda: (mixer_pre(l, t), attention(l, t), merge_and_out(l, t)),
                      lambda: ffn(l, 1), lambda: ple(l, t)]
            if dbg is not None:
                for si in range(dbg):
                    stages[si]()
                dump()
                continue
            for st in stages:
                st()
            if l < depth - 1:
                for c in range(DC):
                    dma(SP, hT_d[c * 128:(c + 1) * 128, g0:g0 + T], h_sb[:, c, :], tuple(h_b[c]), (hT_b[c][t],))
            else:
                for s in range(NS):
                    pN, pN_b = bank()
                    for c in range(DC):
                        sq, sq_b = tmp_b()
                        act(sq[:], h_sb[:, c, ssl(s)], AF.Square, (h_b[c][s],), (sq_b,))
                        mm(pN[:], ones1k[:], sq[:], c == 0, c == DC - 1, (sq_b, const_b), (pN_b,))
                    sd, sd_b = tmp_f()
                    act(sd[:], pN[:], AF.Sqrt, (pN_b,), (sd_b,), scale=1.0, bias=EPS)
                    pg.add(DVE, (lambda sd=sd: nc.vector.reciprocal(out=sd[:], in_=sd[:])), (sd_b,), (sd_b,))
                    for c in range(DC):
                        oi = nxt("ost", 2)
                        ot, ot_b = ost_sb[oi], ost_b[oi]
                        stt(DVE, ot[:], h_sb[:, c, ssl(s)], col(o_final + c), sd[:], ALU.mult, ALU.mult,
                            (h_b[c][s], sd_b, small_b), (ot_b,))
                        dma(SP, outT[c * 128:(c + 1) * 128, g0 + s * SUB:g0 + (s + 1) * SUB], ot[:], (ot_b,), ())
    pg.finish()
    pg.emit()
    import os
    if os.environ.get("KDBG_PHASES"):
        import json
        with open(os.environ["KDBG_PHASES"], "w") as f:
            json.dump({e: [o.phase for o in pg.ops if o.eng == e and not o.dma] for e in Prog.ENGS}, f)
    return nc


def raise_if_none(x):
    if x is None:
        raise RuntimeError("AP.bitcast unavailable")


def pack_small(depth, ffn1_norm, mix_norm, ffn2_norm, ple_norm, final_norm, q_norm, kv_norm, conv_w):
    cols = []
    for l in range(depth):
        for g in (ffn1_norm, mix_norm, ffn2_norm, ple_norm):
            cols.append(np.asarray(g[l], np.float32).reshape(8, 128).T)
    cols.append(np.asarray(final_norm, np.float32).reshape(8, 128).T)
    for l in range(depth):
        cols.append(np.asarray(q_norm[l], np.float32).reshape(3, 128).T)
    for l in range(depth):
        cols.append(np.asarray(kv_norm[l], np.float32).reshape(2, 128).T)
    for l in range(depth):
        cw = np.asarray(conv_w[l], np.float32).reshape(3, 4, 128)
        cols.append(cw.transpose(2, 0, 1).reshape(128, 12))
    invf = (10000.0 ** (-np.arange(0, ROPE, 2, dtype=np.float32) / ROPE)).astype(np.float32)
    iv = np.zeros((128, 1), np.float32)
    iv[0:32, 0] = invf
    iv[32:64, 0] = invf
    cols.append(iv)
    return np.ascontiguousarray(np.concatenate(cols, axis=1), dtype=np.float32)


_NC_CACHE = {}


def run(inputs, S, depth, ncores, T=1024, trace=False, dbg=None):
    key = (S, depth, T, dbg)
    if key not in _NC_CACHE:
        _NC_CACHE[key] = build_program(S, depth, T, dbg=dbg)
    nc = _NC_CACHE[key]
    f = lambda k: np.ascontiguousarray(np.asarray(inputs[k], dtype=np.float32))
    small = pack_small(depth, inputs["ffn1_norm"], inputs["mix_norm"], inputs["ffn2_norm"], inputs["ple_norm"],
                       inputs["final_norm"], inputs["q_norm"], inputs["kv_norm"], inputs["conv_w"])
    shared = {k: f(k) for k in ("ffn1_w_gu", "ffn2_w_gu", "ffn1_w_down", "ffn2_w_down", "w_in", "w_conv_out",
                                "w_uq", "w_ukv", "w_mla_out", "w_o", "w_ple_gate", "w_ple_proj")}
    shared["small"] = small
    x = np.asarray(inputs["x"], np.float32)
    p = np.asarray(inputs["p"], np.float32)
    pos = np.asarray(inputs["positions"]).astype(np.int32)
    in_maps = []
    for b in range(ncores):
        m = dict(shared)
        m["xT"] = np.ascontiguousarray(x[b].T)
        m["pT"] = np.ascontiguousarray(p[:, b].transpose(0, 2, 1))
        m["pos64"] = np.ascontiguousarray(np.broadcast_to(pos[b][None, :], (64, S)))
        in_maps.append(m)
    res = run_bass_kernel_spmd(nc, in_maps, core_ids=list(range(ncores)), trace=trace)
    out = np.stack([np.ascontiguousarray(r["outT"].T) for r in res.results], axis=0)
    return out.astype(np.float32), res


def kernel(**inputs):
    x = inputs["x"]
    B, S, _ = x.shape
    depth = inputs["ffn1_w_gu"].shape[0]
    out, _ = run(inputs, S, depth, B)
    return out
```

```python
import math
import numpy as np
import concourse.bass as bass
import concourse.mybir as mybir
from concourse.bass_utils import run_bass_kernel_spmd

F32 = mybir.dt.float32
I32 = mybir.dt.int32
BF16 = mybir.dt.bfloat16
AF = mybir.ActivationFunctionType
ALU = mybir.AluOpType

D_MODEL = 1024
DC = 8
D_FF = 2816
FC = 22
CONV_DIM = 512
N_HEADS = 8
NOPE = 128
ROPE = 64
VD = 128
Q_LORA = 384
KV_LORA = 256
PLE = 256
EPS = 1e-6
QK = NOPE + ROPE
ATTN_SCALE = QK ** -0.5
IN_COLS = 4288
OFF_B, OFF_C, OFF_V, OFF_Q, OFF_KV, OFF_KR, OFF_GC, OFF_GM = 0, 512, 1024, 1536, 1920, 2176, 2240, 3264
SUB = 512
MAGIC = 12582912.0
TWO_PI_HI = 6.28125
TWO_PI_LO = 2 * math.pi - 6.28125
PI_SAFE = 3.1415925


class Buf:
    __slots__ = ("name", "last_w", "readers", "aliases")

    def __init__(self, name):
        self.name = name
        self.last_w = None
        self.readers = []
        self.aliases = ()


class Op:
    __slots__ = ("eng", "fn", "dma", "deps", "marked", "cnt", "sem_i", "val", "phase")

    def __init__(self, eng, fn, dma):
        self.eng = eng
        self.fn = fn
        self.dma = dma
        self.deps = ()
        self.marked = False
        self.cnt = 0
        self.sem_i = 0
        self.val = 0
        self.phase = ""


class Prog:
    ENGS = ("pe", "act", "dve", "pool", "sp")
    R = 16

    def __init__(self, nc):
        self.nc = nc
        self.ops = []
        self.engobj = {"pe": nc.tensor, "act": nc.scalar, "dve": nc.vector,
                       "pool": nc.gpsimd, "sp": nc.sync}
        self.csem = {e: nc.alloc_semaphore("c_" + e) for e in self.ENGS}
        self.dsem = {q: [nc.alloc_semaphore(f"d_{q}{i}") for i in range(self.R)]
                     for q in ("sp", "pool")}
        self.dhist = {"sp": [], "pool": []}
        self.phase = ""

    def add(self, eng, fn, reads=(), writes=(), dma=False, extra=()):
        op = Op(eng, fn, dma)
        op.phase = self.phase
        deps = {}
        for d in extra:
            deps[id(d)] = d
        for b in reads:
            w = b.last_w
            if w is not None:
                deps[id(w)] = w
        for b in writes:
            for bb in (b,) + tuple(b.aliases):
                w = bb.last_w
                if w is not None:
                    deps[id(w)] = w
                for r in bb.readers:
                    deps[id(r)] = r
        if dma:
            hist = self.dhist[eng]
            k = len(hist)
            op.sem_i = k % self.R
            op.val = 16 * (k // self.R + 1)
            if k >= self.R:
                d = hist[k - self.R]
                deps[id(d)] = d
            hist.append(op)
        dl = []
        for d in deps.values():
            if (not d.dma) and d.eng == "pe" and eng == "pe" and not dma:
                continue
            dl.append(d)
            if not d.dma:
                d.marked = True
        op.deps = dl
        for b in writes:
            for bb in (b,) + tuple(b.aliases):
                bb.last_w = op
                bb.readers = []
        for b in reads:
            if dma:
                b.readers.append(op)
            else:
                rl = b.readers
                for i, r in enumerate(rl):
                    if (not r.dma) and r.eng == eng:
                        rl[i] = op
                        break
                else:
                    rl.append(op)
        self.ops.append(op)
        return op

    def emit(self):
        cnt = {e: 0 for e in self.ENGS}
        seen = {e: {} for e in self.ENGS}
        for op in self.ops:
            E = self.engobj[op.eng]
            sn = seen[op.eng]
            need = {}
            for d in op.deps:
                if d.dma:
                    sem, val = self.dsem[d.eng][d.sem_i], d.val
                else:
                    sem, val = self.csem[d.eng], d.cnt
                if sn.get(sem.num, 0) >= val:
                    continue
                if need.get(sem.num, (None, 0))[1] < val:
                    need[sem.num] = (sem, val)
            for num, (sem, val) in need.items():
                sn[num] = val
                E.wait_ge(sem, val)
            ins = op.fn()
            if op.dma:
                ins.then_inc(self.dsem[op.eng][op.sem_i], 16)
            elif op.marked:
                cnt[op.eng] += 1
                op.cnt = cnt[op.eng]
                ins.then_inc(self.csem[op.eng], 1)

    def finish(self):
        deps = []
        for q in ("sp", "pool"):
            deps += self.dhist[q][-self.R:]
        op = Op("sp", lambda: self.nc.sync.nop(), False)
        op.deps = deps
        self.ops.append(op)


def build_program(S, depth, T=1024, n_wslots=3, dbg=None):
    assert S % T == 0 and T % SUB == 0
    NS = T // SUB
    NT = S // T
    KCH = 1024
    nc = bass.Bass("TRN2", target_bir_lowering=False)
    pg = Prog(nc)
    PE, ACT, DVE, POOL, SP = "pe", "act", "dve", "pool", "sp"

    def din(name, shape, dt=F32):
        return nc.dram_tensor(name, list(shape), dt, kind="ExternalInput").ap()

    xT = din("xT", [D_MODEL, S])
    pT = din("pT", [depth, PLE, S])
    pos64 = din("pos64", [64, S], I32)
    small = din("small", [128, depth * 32 + 8 + depth * 3 + depth * 2 + depth * 12 + 1])
    w_gu = [din("ffn1_w_gu", [depth, D_MODEL, 2 * D_FF]), din("ffn2_w_gu", [depth, D_MODEL, 2 * D_FF])]
    w_dn = [din("ffn1_w_down", [depth, D_FF, D_MODEL]), din("ffn2_w_down", [depth, D_FF, D_MODEL])]
    w_in = din("w_in", [depth, D_MODEL, IN_COLS])
    w_co = din("w_conv_out", [depth, CONV_DIM, D_MODEL])
    w_uq = din("w_uq", [depth, Q_LORA, N_HEADS * QK])
    w_ukv = din("w_ukv", [depth, KV_LORA, N_HEADS * (NOPE + VD)])
    w_mo = din("w_mla_out", [depth, N_HEADS * VD, D_MODEL])
    w_o = din("w_o", [depth, D_MODEL, D_MODEL])
    w_pg = din("w_ple_gate", [depth, D_MODEL, D_MODEL])
    w_pp = din("w_ple_proj", [depth, PLE, D_MODEL])
    outT = nc.dram_tensor("outT", [D_MODEL, S], F32, kind="ExternalOutput").ap()

    hT_d = nc.dram_tensor("hT_scr", [D_MODEL, S], F32).ap()
    kT_d = nc.dram_tensor("kT_scr", [N_HEADS, 128, S], BF16).ap()
    kr_d = nc.dram_tensor("kr_scr", [64, S], BF16).ap()
    v_d = nc.dram_tensor("v_scr", [N_HEADS, 128, S // 128, 128], BF16).ap()
    rq_d = nc.dram_tensor("rq_scr", [depth, 128, 3, N_HEADS, 64], BF16).ap()
    rk_d = nc.dram_tensor("rk_scr", [depth, 128, DC, 64], BF16).ap()
    rq_b = [Buf(f"rqd{l}") for l in range(depth)]
    rk_b = [Buf(f"rkd{l}") for l in range(depth)]
    cos_d = nc.dram_tensor("cos_scr", [64, S], F32).ap()
    sin_d = nc.dram_tensor("sin_scr", [64, S], F32).ap()

    hT_b = [[Buf(f"hTd{c}_{t}") for t in range(NT)] for c in range(DC)]
    kT_b = [[Buf(f"kTd{h}_{t}") for t in range(NT)] for h in range(N_HEADS)]
    kr_b = [Buf(f"krd{t}") for t in range(NT)]
    v_b = [[Buf(f"vd{h}_{t}") for t in range(NT)] for h in range(N_HEADS)]
    cs_b = [Buf(f"csd{t}") for t in range(NT)]

    def sb(name, shape, dt):
        return nc.alloc_sbuf_tensor("sb_" + name, list(shape), dt)

    ncol_small = small.shape[1]
    small_sb = sb("small", [128, ncol_small], F32)
    small_b = Buf("small")
    o_gain = 0
    o_final = depth * 32
    o_qn = o_final + 8
    o_kvn = o_qn + depth * 3
    o_cw = o_kvn + depth * 2
    o_invf = o_cw + depth * 12

    ones1k = sb("ones1k", [128, 128], BF16)
    ones256 = sb("ones256", [128, 128], BF16)
    ones1 = sb("ones1", [128, 128], BF16)
    const_b = Buf("consts")

    h_sb = sb("h", [128, DC, T], F32)
    h_b = [[Buf(f"h{c}_{s}") for s in range(NS)] for c in range(DC)]
    xn_sb = sb("xn", [128, DC, T], BF16)
    xn_b = [[Buf(f"xn{c}_{s}") for s in range(NS)] for c in range(DC)]

    hid_sb = sb("hid", [128, FC, T], BF16)
    hid_b = [[Buf(f"hid{j}_{s}") for s in range(NS)] for j in range(FC)]
    hid_flat = hid_sb[:].rearrange("p a b -> p (a b)")

    def hid_view(off_chunks, n):
        return hid_flat[:, off_chunks * T:(off_chunks + n) * T]

    qnT_v = [hid_view(hd, 1) for hd in range(N_HEADS)]
    qrT_v = [hid_view(8 + hd, 1) for hd in range(N_HEADS)]
    qn_v = [hid_view(16 + c, 1) for c in range(3)]
    kvn_v = [hid_view(19 + c, 1) for c in range(2)]
    qnT_b = [[Buf(f"qnT{h}_{s}") for s in range(NS)] for h in range(N_HEADS)]
    qrT_b = [[Buf(f"qrT{h}_{s}") for s in range(NS)] for h in range(N_HEADS)]
    qn_b = [[Buf(f"qn{c}_{s}") for s in range(NS)] for c in range(3)]
    kvn_b = [[Buf(f"kvn{c}_{s}") for s in range(NS)] for c in range(2)]
    mix_bufs = [b for l in (qnT_b, qrT_b, qn_b, kvn_b) for row in l for b in row]
    hid_bufs = [b for row in hid_b for b in row]
    for b in mix_bufs:
        b.aliases = tuple(hid_bufs)
    for b in hid_bufs:
        b.aliases = tuple(mix_bufs)

    r_sb = sb("r", [128, 4, T], BF16)
    r_b = [[Buf(f"r{c}_{s}") for s in range(NS)] for c in range(4)]
    o_sb = sb("osb", [128, N_HEADS, T], BF16)
    o_b = [[Buf(f"o{h}_{s}") for s in range(NS)] for h in range(N_HEADS)]

    NKV = 3
    kc_sb = [sb(f"kc{i}", [128, KCH], BF16) for i in range(NKV)]
    kc_b = [Buf(f"kc{i}") for i in range(NKV)]
    vc_sb = [sb(f"vc{i}", [128, KCH // 128, 128], BF16) for i in range(NKV)]
    vc_b = [Buf(f"vc{i}") for i in range(NKV)]
    kr_sb = sb("krall", [128, S], BF16)
    kr_sb_b = Buf("krall")
    mc_elems = max(DC * SUB, 4 * T)
    mc_sb = sb("mergecs", [128, mc_elems], BF16)
    merged_sb = mc_sb[:, 0:DC * SUB].rearrange("p (c n) -> p c n", n=SUB)
    merged_b = [Buf(f"mg{c}") for c in range(DC)]
    cs_sb = mc_sb.bitcast(F32)[0:64, 0:2 * T].rearrange("p (a t) -> p a t", a=2)
    cs_sb_b = Buf("cossin")
    cs_sb_b.aliases = tuple(merged_b)
    for b_ in merged_b:
        b_.aliases = (cs_sb_b,)

    WSLOT = 3584
    w_sb = [sb(f"w{i}", [128, WSLOT], BF16) for i in range(n_wslots)]
    w_b = [[Buf(f"w{i}_{k}") for k in range(4)] for i in range(n_wslots)]
    waux_sb = sb("waux", [128, 2, 1024], BF16)
    waux_b = Buf("waux")
    kp_sb = sb("kstp", [128, 2, T], BF16)
    kst_sb = [kp_sb[:, i, :] for i in range(2)]
    kst_b = [Buf(f"kst{i}") for i in range(2)]
    pbuf_sb = kp_sb
    pbuf_b = Buf("pbuf")
    pbuf_b.aliases = tuple(kst_b)
    for b_ in kst_b:
        b_.aliases = (pbuf_b,)

    NPR = 3
    p_sb = [sb(f"P{i}", [128, SUB], BF16) for i in range(NPR)]
    p_b = [Buf(f"P{i}") for i in range(NPR)]
    NPD = 2
    pd_sb = [sb(f"Pd{i}", [128, SUB], BF16) for i in range(NPD)]
    pd_b = [Buf(f"Pd{i}") for i in range(NPD)]

    NTF = 5
    tf_sb = [sb(f"tf{i}", [128, SUB], F32) for i in range(NTF)]
    tf_b = [Buf(f"tf{i}") for i in range(NTF)]
    NTB = 2
    tb_sb = [sb(f"tb{i}", [128, SUB], BF16) for i in range(NTB)]
    tb_b = [Buf(f"tb{i}") for i in range(NTB)]
    NZ = 1
    zbuf_sb = [sb(f"z{i}", [128, 2 + T], F32) for i in range(NZ)]
    zbuf_b = [Buf(f"z{i}") for i in range(NZ)]
    carry_sb = sb("carry", [128, 4, 2], F32)
    carry_b = [Buf(f"carry{c}") for c in range(4)]
    vst_sb = [sb(f"vst{i}", [128, 1024], BF16) for i in range(2)]
    vst_b = [Buf(f"vst{i}") for i in range(2)]
    krst_sb = hid_view(21, 1)[0:64, :]
    krst_b = Buf("krst")
    krst_b.aliases = tuple(hid_bufs)
    for b_ in hid_bufs:
        b_.aliases = b_.aliases + (krst_b,)

    ost_sb = [sb(f"ost{i}", [128, SUB], F32) for i in range(2)]
    ost_b = [Buf(f"ost{i}") for i in range(2)]
    posi_sb = sb("posi", [64, SUB], I32)
    posi_b = Buf("posi")
    ps = [nc.alloc_psum_tensor(f"ps{i}", [128, SUB], F32) for i in range(8)]
    ps_b = [Buf(f"ps{i}") for i in range(8)]

    print("sbuf bytes remaining per partition:", nc.sbuf_bytes_remaining)

    rot = {"tf": 0, "tb": 0, "w": 0, "ps": 0, "p": 0, "pd": 0, "kv": 0, "z": 0, "kst": 0, "vst": 0, "sb": 0, "ost": 0}

    def nxt(key, n):
        i = rot[key]
        rot[key] = (i + 1) % n
        return i

    def tmp_f():
        i = nxt("tf", NTF)
        return tf_sb[i], tf_b[i]

    def tmp_b():
        i = nxt("tb", NTB)
        return tb_sb[i], tb_b[i]

    def bank():
        i = nxt("ps", 8)
        return ps[i], ps_b[i]

    wpending = {}

    def wslot():
        i = nxt("w", n_wslots)
        pend = {}
        for b in w_b[i]:
            if b.last_w is not None:
                pend[id(b.last_w)] = b.last_w
            for r in b.readers:
                pend[id(r)] = r
            b.last_w = None
            b.readers = []
        for b in w_b[i]:
            wpending[id(b)] = tuple(pend.values())
        return w_sb[i], w_b[i]

    def mm(out, lhsT, rhs, start, stop, reads, writes):
        pg.add(PE, lambda: nc.tensor.matmul(out, lhsT, rhs, start=start, stop=stop), reads, writes)

    def act(out, in_, func, reads, writes, scale=1.0, bias=0.0):
        pg.add(ACT, lambda: nc.scalar.activation(out=out, in_=in_, func=func, bias=bias, scale=scale),
               reads, writes)

    def tt(eng, out, in0, in1, op, reads, writes):
        E = pg.engobj[eng]
        pg.add(eng, lambda: E.tensor_tensor(out=out, in0=in0, in1=in1, op=op), reads, writes)

    def ts(eng, out, in0, s1, s2, op0, op1, reads, writes):
        E = pg.engobj[eng]
        if op1 is None:
            pg.add(eng, lambda: E.tensor_scalar(out=out, in0=in0, scalar1=s1, scalar2=None, op0=op0),
                   reads, writes)
        else:
            pg.add(eng, lambda: E.tensor_scalar(out=out, in0=in0, scalar1=s1, scalar2=s2, op0=op0, op1=op1),
                   reads, writes)

    def stt(eng, out, in0, scalar, in1, op0, op1, reads, writes):
        E = pg.engobj[eng]
        pg.add(eng, lambda: E.scalar_tensor_tensor(out=out, in0=in0, scalar=scalar, in1=in1, op0=op0, op1=op1),
               reads, writes)

    def cp(eng, out, in_, reads, writes):
        E = pg.engobj[eng]
        if eng == ACT:
            pg.add(eng, lambda: E.copy(out=out, in_=in_), reads, writes)
        else:
            pg.add(eng, lambda: E.tensor_copy(out=out, in_=in_), reads, writes)

    def dma(q, out, in_, reads, writes, extra=()):
        E = pg.engobj[q]
        pg.add(q, lambda: E.dma_start(out=out, in_=in_), reads, writes, dma=True, extra=extra)

    def wload(dst, src, dst_b):
        dma(POOL, dst, src, (), (dst_b,), extra=wpending.get(id(dst_b), ()))

    def wview(w3, l, c0, ncols):
        return w3[l].rearrange("(kc p) n -> p kc n", p=128)[:, :, c0:c0 + ncols]

    def slot_view(slot, kc, ncols, off=0):
        return slot[:, off:off + kc * ncols].rearrange("p (k n) -> p k n", n=ncols)

    def col(off):
        return small_sb[:, off:off + 1]

    def ssl(s):
        return slice(s * SUB, (s + 1) * SUB)

    dma(SP, small_sb[:], small, (), (small_b,))
    pg.add(DVE, lambda: nc.vector.memset(ones1k[:], 1.0 / 1024.0), (), (const_b,))
    pg.add(DVE, lambda: nc.vector.memset(ones256[:], 1.0 / 256.0), (), (const_b,))
    pg.add(DVE, lambda: nc.vector.memset(ones1[:], 1.0), (), (const_b,))
    for i in range(NPD):
        pg.add(DVE, (lambda i=i: nc.vector.memset(pd_sb[i][:], 0.0)), (), (pd_b[i],))
    pg.add(DVE, lambda: nc.vector.memset(kr_sb[64:128, :], 0.0), (), (kr_sb_b,))
    for i in range(n_wslots):
        pg.add(DVE, (lambda i=i: nc.vector.memset(w_sb[i][:], 0.0)), (), tuple(w_b[i]))

    for l in range(depth):
        slot, sbs = wslot()
        src = slot[:, 0:1536].rearrange("p (k h c) -> p k h c", k=3, h=N_HEADS)
        dst = slot[:, 1536:3072].rearrange("p (k h c) -> p k h c", k=3, h=N_HEADS)
        wq = w_uq[l].rearrange("(kc p) (h c) -> p kc h c", p=128, c=QK)
        for kc in range(3):
            wload(src[:, kc], wq[:, kc, :, NOPE:QK], sbs[0])
        ts(DVE, dst[:, :, :, 0:32], src[:, :, :, 32:64], -1.0, None, ALU.mult, None, (sbs[0],), (sbs[1],))
        cp(DVE, dst[:, :, :, 32:64], src[:, :, :, 0:32], (sbs[0], sbs[1]), (sbs[1],))
        dma(SP, rq_d[l], dst, (sbs[1],), (rq_b[l],))
        slot, sbs = wslot()
        src = slot[:, 0:512].rearrange("p (k c) -> p k c", k=DC)
        dst = slot[:, 512:1024].rearrange("p (k c) -> p k c", k=DC)
        wload(src, wview(w_in, l, OFF_KR, 64), sbs[0])
        ts(DVE, dst[:, :, 0:32], src[:, :, 32:64], -1.0, None, ALU.mult, None, (sbs[0],), (sbs[1],))
        cp(DVE, dst[:, :, 32:64], src[:, :, 0:32], (sbs[0], sbs[1]), (sbs[1],))
        dma(SP, rk_d[l], dst, (sbs[1],), (rk_b[l],))

    for t in range(NT):
        tsl = slice(t * T, (t + 1) * T)
        for s in range(NS):
            g0 = t * T + s * SUB
            a_t, a_b = tmp_f()
            k_t, k_b = tmp_f()
            r_t, r_bb = tmp_f()
            c_t, c_b = tmp_f()
            dma(SP, posi_sb[:], pos64[:, g0:g0 + SUB], (), (posi_b,))
            cp(DVE, a_t[0:64, :], posi_sb[:], (posi_b,), (a_b,))
            ts(DVE, a_t[0:64, :], a_t[0:64, :], small_sb[0:64, o_invf:o_invf + 1], None, ALU.mult, None,
               (a_b, small_b), (a_b,))
            ts(DVE, k_t[0:64, :], a_t[0:64, :], 1.0 / (2 * math.pi), MAGIC, ALU.mult, ALU.add, (a_b,), (k_b,))
            ts(DVE, k_t[0:64, :], k_t[0:64, :], MAGIC, None, ALU.subtract, None, (k_b,), (k_b,))
            stt(DVE, r_t[0:64, :], k_t[0:64, :], -TWO_PI_HI, a_t[0:64, :], ALU.mult, ALU.add, (k_b, a_b), (r_bb,))
            stt(DVE, r_t[0:64, :], k_t[0:64, :], -TWO_PI_LO, r_t[0:64, :], ALU.mult, ALU.add, (k_b, r_bb), (r_bb,))
            ts(DVE, c_t[0:64, :], r_t[0:64, :], math.pi / 2, None, ALU.add, None, (r_bb,), (c_b,))
            ts(DVE, k_t[0:64, :], c_t[0:64, :], math.pi, None, ALU.is_gt, None, (c_b,), (k_b,))
            stt(DVE, c_t[0:64, :], k_t[0:64, :], -2 * math.pi, c_t[0:64, :], ALU.mult, ALU.add, (k_b, c_b), (c_b,))
            ts(DVE, r_t[0:64, :], r_t[0:64, :], -PI_SAFE, PI_SAFE, ALU.max, ALU.min, (r_bb,), (r_bb,))
            ts(DVE, c_t[0:64, :], c_t[0:64, :], -PI_SAFE, PI_SAFE, ALU.max, ALU.min, (c_b,), (c_b,))
            act(a_t[0:64, :], r_t[0:64, :], AF.Sin, (r_bb,), (a_b,))
            act(k_t[0:64, :], c_t[0:64, :], AF.Sin, (c_b,), (k_b,))
            dma(SP, sin_d[:, g0:g0 + SUB], a_t[0:64, :], (a_b,), (cs_b[t],))
            dma(SP, cos_d[:, g0:g0 + SUB], k_t[0:64, :], (k_b,), (cs_b[t],))

    def rms_norm_tile(l, which, dst_sb, dst_bufs):
        for s in range(NS):
            pN, pN_b = bank()
            for c in range(DC):
                sq, sq_b = tmp_b()
                act(sq[:], h_sb[:, c, ssl(s)], AF.Square, (h_b[c][s],), (sq_b,))
                mm(pN[:], ones1k[:], sq[:], c == 0, c == DC - 1, (sq_b, const_b), (pN_b,))
            sd, sd_b = tmp_f()
            act(sd[:], pN[:], AF.Ln, (pN_b,), (sd_b,), scale=1.0, bias=EPS)
            act(sd[:], sd[:], AF.Exp, (sd_b,), (sd_b,), scale=-0.5)
            for c in range(DC):
                if which == "final":
                    gc = col(o_final + c)
                else:
                    gc = col(o_gain + l * 32 + which * 8 + c)
                stt(DVE, dst_sb[:, c, ssl(s)], h_sb[:, c, ssl(s)], gc, sd[:], ALU.mult, ALU.mult,
                    (h_b[c][s], sd_b, small_b), (dst_bufs[c][s],))

    def ffn(l, f):
        pg.phase = "ffn_norm"
        rms_norm_tile(l, 0 if f == 0 else 2, xn_sb, xn_b)
        pg.phase = "ffn_gu"
        for jp in range(0, FC, 2):
            pcs = []
            for j in (jp, jp + 1):
                slot, sbs = wslot()
                sv = slot_view(slot, DC, 256)
                wload(sv[:, :, 0:128], wview(w_gu[f], l, j * 128, 128), sbs[0])
                wload(sv[:, :, 128:256], wview(w_gu[f], l, D_FF + j * 128, 128), sbs[1])
                pcs.append((j, sv, sbs))
            for s in range(NS):
                for (j, sv, sbs) in pcs:
                    pgk, pg_b = bank()
                    pu, pu_b = bank()
                    for kc in range(DC):
                        mm(pgk[:], sv[:, kc, 0:128], xn_sb[:, kc, ssl(s)], kc == 0, kc == DC - 1,
                           (sbs[0], xn_b[kc][s]), (pg_b,))
                    for kc in range(DC):
                        mm(pu[:], sv[:, kc, 128:256], xn_sb[:, kc, ssl(s)], kc == 0, kc == DC - 1,
                           (sbs[1], xn_b[kc][s]), (pu_b,))
                    sg, sg_b = tmp_f()
                    act(sg[:], pgk[:], AF.Silu, (pg_b,), (sg_b,))
                    tt(DVE, hid_sb[:, j, ssl(s)], sg[:], pu[:], ALU.mult, (sg_b, pu_b), (hid_b[j][s],))
        pg.phase = "ffn_down"
        for c in range(DC):
            slot, sbs = wslot()
            sv = slot_view(slot, FC, 128)
            wload(sv, wview(w_dn[f], l, c * 128, 128), sbs[0])
            for s in range(NS):
                po, po_b = bank()
                for kc in range(FC):
                    mm(po[:], sv[:, kc, :], hid_sb[:, kc, ssl(s)], kc == 0, kc == FC - 1,
                       (sbs[0], hid_b[kc][s]), (po_b,))
                stt(DVE, h_sb[:, c, ssl(s)], po[:], 0.5, h_sb[:, c, ssl(s)], ALU.mult, ALU.add,
                    (po_b, h_b[c][s]), (h_b[c][s],))

    def small_norm(srcs, src_bs, nch, ones_t, scale, gain_off, dst_views, dst_bufs, s):
        pN, pN_b = bank()
        for c in range(nch):
            sq, sq_b = tmp_b()
            act(sq[:], srcs[c][:], AF.Square, (src_bs[c],), (sq_b,))
            mm(pN[:], ones_t[:], sq[:], c == 0, c == nch - 1, (sq_b, const_b), (pN_b,))
        sd, sd_b = tmp_f()
        act(sd[:], pN[:], AF.Ln, (pN_b,), (sd_b,), scale=scale, bias=EPS)
        act(sd[:], sd[:], AF.Exp, (sd_b,), (sd_b,), scale=-0.5)
        for c in range(nch):
            stt(DVE, dst_views[c][:, ssl(s)], srcs[c][:], col(gain_off + c), sd[:], ALU.mult, ALU.mult,
                (src_bs[c], sd_b, small_b), (dst_bufs[c][s],))

    def rope_combine(px, px_b, pr, pr_b, s, out_ap, out_b):
        t1, t1_b = tmp_f()
        t2, t2_b = tmp_f()
        tt(DVE, t1[0:64, :], px[0:64, :], cs_sb[:, 0, ssl(s)], ALU.mult, (px_b, cs_sb_b), (t1_b,))
        tt(DVE, t2[0:64, :], pr[0:64, :], cs_sb[:, 1, ssl(s)], ALU.mult, (pr_b, cs_sb_b), (t2_b,))
        tt(DVE, out_ap, t1[0:64, :], t2[0:64, :], ALU.add, (t1_b, t2_b), (out_b,))

    def build_rot(slot3, kc, c_x1, c_x2, c_dst, sb_src, sb_dst):
        ts(DVE, slot3[:, 0:kc, c_dst:c_dst + 32], slot3[:, 0:kc, c_x2:c_x2 + 32], -1.0, None, ALU.mult, None,
           (sb_src,), (sb_dst,))
        cp(DVE, slot3[:, 0:kc, c_dst + 32:c_dst + 64], slot3[:, 0:kc, c_x1:c_x1 + 32], (sb_src, sb_dst), (sb_dst,))

    def mixer_pre(l, t):
        g0 = t * T
        pg.phase = "mix_norm"
        rms_norm_tile(l, 1, xn_sb, xn_b)
        pg.phase = "mix_conv"
        dma(SP, cs_sb[:, 0, :], cos_d[:, g0:g0 + T], (cs_b[t],), (cs_sb_b,))
        dma(SP, cs_sb[:, 1, :], sin_d[:, g0:g0 + T], (cs_b[t],), (cs_sb_b,))
        for c in range(4):
            slot, sbs = wslot()
            sv = slot_view(slot, DC, 384)
            wload(sv[:, :, 0:128], wview(w_in, l, OFF_B + c * 128, 128), sbs[0])
            wload(sv[:, :, 128:256], wview(w_in, l, OFF_C + c * 128, 128), sbs[1])
            wload(sv[:, :, 256:384], wview(w_in, l, OFF_V + c * 128, 128), sbs[2])
            zi = nxt("z", NZ)
            z, z_b = zbuf_sb[zi], zbuf_b[zi]
            if t == 0:
                pg.add(DVE, (lambda z=z: nc.vector.memset(z[:, 0:2], 0.0)), (), (z_b,))
            else:
                cp(DVE, z[:, 0:2], carry_sb[:, c, :], (carry_b[c],), (z_b,))
            for s in range(NS):
                pB, pB_b = bank()
                pC, pC_b = bank()
                pV, pV_b = bank()
                for (pp_, pp_b, o, k) in ((pB, pB_b, 0, 0), (pC, pC_b, 128, 1), (pV, pV_b, 256, 2)):
                    for kc in range(DC):
                        mm(pp_[:], sv[:, kc, o:o + 128], xn_sb[:, kc, ssl(s)], kc == 0, kc == DC - 1,
                           (sbs[k], xn_b[kc][s]), (pp_b,))
                cs_, cs_bb = tmp_f()
                cp(ACT, cs_[:], pC[:], (pC_b,), (cs_bb,))
                zo = 2 + s * SUB
                tt(DVE, z[:, zo:zo + SUB], cs_[:], pV[:], ALU.mult, (cs_bb, pV_b), (z_b,))
                y, y_b = tmp_f()
                cw = o_cw + l * 12
                ts(DVE, y[:], z[:, zo:zo + SUB], col(cw + 2 * 4 + c), None, ALU.mult, None, (z_b, small_b), (y_b,))
                stt(DVE, y[:], z[:, zo - 1:zo - 1 + SUB], col(cw + 1 * 4 + c), y[:], ALU.mult, ALU.add,
                    (z_b, y_b, small_b), (y_b,))
                stt(DVE, y[:], z[:, zo - 2:zo - 2 + SUB], col(cw + 0 * 4 + c), y[:], ALU.mult, ALU.add,
                    (z_b, y_b, small_b), (y_b,))
                tt(DVE, r_sb[:, c, ssl(s)], y[:], pB[:], ALU.mult, (y_b, pB_b), (r_b[c][s],))
            cp(DVE, carry_sb[:, c, :], z[:, T:T + 2], (z_b,), (carry_b[c],))
        pg.phase = "mix_qc"
        slot, sbs = wslot()
        sv = slot_view(slot, DC, 384)
        wload(sv, wview(w_in, l, OFF_Q, 384), sbs[0])
        for s in range(NS):
            srcs, src_bs = [], []
            for c in range(3):
                pq, pq_b = bank()
                for kc in range(DC):
                    mm(pq[:], sv[:, kc, c * 128:(c + 1) * 128], xn_sb[:, kc, ssl(s)], kc == 0, kc == DC - 1,
                       (sbs[0], xn_b[kc][s]), (pq_b,))
                qf, qf_b = tmp_f()
                cp(ACT, qf[:], pq[:], (pq_b,), (qf_b,))
                srcs.append(qf)
                src_bs.append(qf_b)
            small_norm(srcs, src_bs, 3, ones256, 256.0 / 384.0, o_qn + l * 3, qn_v, qn_b, s)
        pg.phase = "mix_kvc"
        slot, sbs = wslot()
        sv = slot_view(slot, DC, 448)
        wload(sv[:, :, 0:320], wview(w_in, l, OFF_KV, 320), sbs[0])
        dma(POOL, sv[:, :, 320:384], rk_d[l], (rk_b[l],), (sbs[1],), extra=wpending.get(id(sbs[1]), ()))
        for s in range(NS):
            srcs, src_bs = [], []
            for c in range(2):
                pk, pk_b = bank()
                for kc in range(DC):
                    mm(pk[:], sv[:, kc, c * 128:(c + 1) * 128], xn_sb[:, kc, ssl(s)], kc == 0, kc == DC - 1,
                       (sbs[0], xn_b[kc][s]), (pk_b,))
                kf, kf_b = tmp_f()
                cp(ACT, kf[:], pk[:], (pk_b,), (kf_b,))
                srcs.append(kf)
                src_bs.append(kf_b)
            small_norm(srcs, src_bs, 2, ones256, 1.0, o_kvn + l * 2, kvn_v, kvn_b, s)
            px, px_b = bank()
            pr, pr_b = bank()
            for kc in range(DC):
                mm(px[:], sv[:, kc, 256:384], xn_sb[:, kc, ssl(s)], kc == 0, kc == DC - 1,
                   (sbs[0], sbs[1], xn_b[kc][s]), (px_b,))
            for kc in range(DC):
                mm(pr[:], sv[:, kc, 320:448], xn_sb[:, kc, ssl(s)], kc == 0, kc == DC - 1,
                   (sbs[1], xn_b[kc][s]), (pr_b,))
            rope_combine(px, px_b, pr, pr_b, s, krst_sb[:, ssl(s)], krst_b)
        dma(SP, kr_d[:, g0:g0 + T], krst_sb, (krst_b,), (kr_b[t],))
        pg.phase = "mix_uq"
        for hd in range(N_HEADS):
            pg.add(ACT, (lambda hd=hd: nc.scalar.memzero(qrT_v[hd][64:128, :])), (), tuple(qrT_b[hd]))
        for hd in range(N_HEADS):
            slot, sbs = wslot()
            sv = slot_view(slot, 3, 320)
            wload(sv[:, :, 0:192], wview(w_uq, l, hd * QK, QK), sbs[0])
            dma(POOL, sv[:, :, 192:256], rq_d[l][:, :, hd, :], (rq_b[l],), (sbs[1],),
                extra=wpending.get(id(sbs[1]), ()))
            for s in range(NS):
                pn, pn_b = bank()
                px, px_b = bank()
                pr, pr_b = bank()
                for kc in range(3):
                    mm(pn[:], sv[:, kc, 0:128], qn_v[kc][:, ssl(s)], kc == 0, kc == 2, (sbs[0], qn_b[kc][s]), (pn_b,))
                for kc in range(3):
                    mm(px[:], sv[:, kc, 128:256], qn_v[kc][:, ssl(s)], kc == 0, kc == 2,
                       (sbs[0], sbs[1], qn_b[kc][s]), (px_b,))
                for kc in range(3):
                    mm(pr[:], sv[:, kc, 192:320], qn_v[kc][:, ssl(s)], kc == 0, kc == 2,
                       (sbs[1], qn_b[kc][s]), (pr_b,))
                cp(ACT, qnT_v[hd][:, ssl(s)], pn[:], (pn_b,), (qnT_b[hd][s],))
                rope_combine(px, px_b, pr, pr_b, s, qrT_v[hd][0:64, ssl(s)], qrT_b[hd][s])
        pg.phase = "mix_uk"
        for hd in range(N_HEADS):
            slot, sbs = wslot()
            sv = slot_view(slot, 2, 128)
            wload(sv, wview(w_ukv, l, hd * 256, 128), sbs[0])
            ki = nxt("kst", 2)
            for s in range(NS):
                pk, pk_b = bank()
                for kc in range(2):
                    mm(pk[:], sv[:, kc, :], kvn_v[kc][:, ssl(s)], kc == 0, kc == 1, (sbs[0], kvn_b[kc][s]), (pk_b,))
                cp(ACT, kst_sb[ki][:, ssl(s)], pk[:], (pk_b,), (kst_b[ki],))
            dma(SP, kT_d[hd, :, g0:g0 + T], kst_sb[ki], (kst_b[ki],), (kT_b[hd][t],))
        pg.phase = "mix_v"
        wv = w_ukv[l].rearrange("(kc p) (h two d) -> p kc h two d", p=128, two=2, d=128)
        for kc in range(2):
            wload(waux_sb[:, kc, :].rearrange("p (h d) -> p h d", d=128), wv[:, kc, :, 1, :], waux_b)
        for blk in range(T // 128):
            s = blk // 4
            vi = nxt("vst", 2)
            bsl = slice(blk * 128, (blk + 1) * 128)
            for half in range(2):
                pv, pv_b = bank()
                for kc in range(2):
                    mm(pv[:], kvn_v[kc][:, bsl], waux_sb[:, kc, half * 512:(half + 1) * 512], kc == 0, kc == 1,
                       (waux_b, kvn_b[kc][s]), (pv_b,))
                if half == 0:
                    cp(ACT, vst_sb[vi][:, 0:512], pv[:], (pv_b,), (vst_b[vi],))
                else:
                    cp(DVE, vst_sb[vi][:, 512:1024], pv[:], (pv_b,), (vst_b[vi],))
            gb = (g0 // 128) + blk
            dma(SP, v_d[:, :, gb, :].rearrange("h p d -> p h d"),
                vst_sb[vi][:].rearrange("p (h d) -> p h d", d=128),
                (vst_b[vi],), tuple(v_b[hd][t] for hd in range(N_HEADS)))

    def attention(l, t):
        pg.phase = "attn"
        g0 = t * T
        nk = g0 + T
        dma(SP, kr_sb[0:64, 0:nk], kr_d[:, 0:nk], tuple(kr_b[:t + 1]), (kr_sb_b,))
        items = []
        for hd in range(N_HEADS):
            for s in range(NS):
                q0 = g0 + s * SUB
                nkb = (q0 + SUB) // 128
                for kb in range(nkb):
                    items.append((hd, s, kb, nkb))
        state = {}

        def emit_scores(it):
            hd, s, kb, nkb = it
            q0 = g0 + s * SUB
            k0 = kb * 128
            if kb % (KCH // 128) == 0:
                ch = k0 // KCH
                ci = nxt("kv", NKV)
                kend = min((ch + 1) * KCH, q0 + SUB)
                klen = kend - ch * KCH
                tl = tuple(range(ch * KCH // T, (kend - 1) // T + 1))
                dma(SP, kc_sb[ci][:, 0:klen], kT_d[hd, :, ch * KCH:kend],
                    tuple(kT_b[hd][tt_] for tt_ in tl), (kc_b[ci],))
                dma(SP, vc_sb[ci][:, 0:klen // 128, :], v_d[hd, :, ch * (KCH // 128):kend // 128, :],
                    tuple(v_b[hd][tt_] for tt_ in tl), (vc_b[ci],))
                state["ci"] = ci
            ci = state["ci"]
            qoff = max(0, k0 - q0)
            N = SUB - qoff
            kl = k0 % KCH
            bi = nxt("sb", 4)
            pS, pS_b = ps[bi], ps_b[bi]
            qsl = slice(s * SUB + qoff, (s + 1) * SUB)
            mm(pS[:, 0:N], kc_sb[ci][:, kl:kl + 128], qnT_v[hd][:, qsl], True, False,
               (kc_b[ci], qnT_b[hd][s]), (pS_b,))
            mm(pS[:, 0:N], kr_sb[:, k0:k0 + 128], qrT_v[hd][:, qsl], False, True,
               (kr_sb_b, qrT_b[hd][s]), (pS_b,))
            return (ci, kl, qoff, N, pS, pS_b)

        def emit_rest(it, sc):
            hd, s, kb, nkb = it
            ci, kl, qoff, N, pS, pS_b = sc
            q0 = g0 + s * SUB
            diag = kb * 128 >= q0
            par = (hd * NS + s) % 2
            pO, pO_b = ps[4 + 2 * par], ps_b[4 + 2 * par]
            pL, pL_b = ps[5 + 2 * par], ps_b[5 + 2 * par]
            if not diag:
                pi = nxt("p", NPR)
                P, P_b = p_sb[pi], p_b[pi]
                act(P[:, 0:N], pS[:, 0:N], AF.Exp, (pS_b,), (P_b,), scale=ATTN_SCALE)
            else:
                pi = nxt("pd", NPD)
                P, P_b = pd_sb[pi], pd_b[pi]
                act(P[0:64, 0:N], pS[0:64, 0:N], AF.Exp, (pS_b,), (P_b,), scale=ATTN_SCALE)
                act(P[64:128, 64:N], pS[64:128, 64:N], AF.Exp, (pS_b,), (P_b,), scale=ATTN_SCALE)
            mm(pO[:, qoff:SUB], vc_sb[ci][:, kl // 128, :], P[:, 0:N], kb == 0, kb == nkb - 1,
               (vc_b[ci], P_b), (pO_b,))
            mm(pL[:, qoff:SUB], ones1[:], P[:, 0:N], kb == 0, kb == nkb - 1, (const_b, P_b), (pL_b,))
            if kb == nkb - 1:
                rl, rl_b = tmp_f()
                pg.add(DVE, (lambda rl=rl, pL=pL: nc.vector.reciprocal(out=rl[:], in_=pL[:])), (pL_b,), (rl_b,))
                tt(DVE, o_sb[:, hd, ssl(s)], pO[:], rl[:], ALU.mult, (pO_b, rl_b), (o_b[hd][s],))

        DEPTH_SW = 2
        scs = []
        for i in range(min(DEPTH_SW, len(items))):
            scs.append(emit_scores(items[i]))
        for i, it in enumerate(items):
            if i + DEPTH_SW < len(items):
                scs.append(emit_scores(items[i + DEPTH_SW]))
            emit_rest(it, scs[i])

    def merge_and_out(l, t):
        for s in range(NS):
            pg.phase = "merge"
            for c in range(DC):
                slot, sbs = wslot()
                sv = slot_view(slot, 28, 128)
                wload(sv[:, 0:8, :], wview(w_in, l, OFF_GC + c * 128, 128), sbs[0])
                wload(sv[:, 8:16, :], wview(w_in, l, OFF_GM + c * 128, 128), sbs[1])
                wload(sv[:, 16:20, :], wview(w_co, l, c * 128, 128), sbs[2])
                wload(sv[:, 20:28, :], wview(w_mo, l, c * 128, 128), sbs[3])
                pgc, pgc_b = bank()
                pyc, pyc_b = bank()
                pgm, pgm_b = bank()
                pym, pym_b = bank()
                for kc in range(DC):
                    mm(pgc[:], sv[:, kc, :], xn_sb[:, kc, ssl(s)], kc == 0, kc == DC - 1, (sbs[0], xn_b[kc][s]), (pgc_b,))
                for kc in range(4):
                    mm(pyc[:], sv[:, 16 + kc, :], r_sb[:, kc, ssl(s)], kc == 0, kc == 3, (sbs[2], r_b[kc][s]), (pyc_b,))
                for kc in range(DC):
                    mm(pgm[:], sv[:, 8 + kc, :], xn_sb[:, kc, ssl(s)], kc == 0, kc == DC - 1, (sbs[1], xn_b[kc][s]), (pgm_b,))
                for kc in range(DC):
                    mm(pym[:], sv[:, 20 + kc, :], o_sb[:, kc, ssl(s)], kc == 0, kc == DC - 1, (sbs[3], o_b[kc][s]), (pym_b,))
                s1, s1_b = tmp_f()
                s2, s2_b = tmp_f()
                act(s1[:], pgc[:], AF.Sigmoid, (pgc_b,), (s1_b,))
                act(s2[:], pgm[:], AF.Sigmoid, (pgm_b,), (s2_b,))
                tt(DVE, s1[:], s1[:], pyc[:], ALU.mult, (s1_b, pyc_b), (s1_b,))
                tt(DVE, s2[:], s2[:], pym[:], ALU.mult, (s2_b, pym_b), (s2_b,))
                tt(DVE, merged_sb[:, c, :], s1[:], s2[:], ALU.add, (s1_b, s2_b), (merged_b[c],))
            pg.phase = "w_o"
            for c in range(DC):
                slot, sbs = wslot()
                sv = slot_view(slot, DC, 128)
                wload(sv, wview(w_o, l, c * 128, 128), sbs[0])
                po, po_b = bank()
                for kc in range(DC):
                    mm(po[:], sv[:, kc, :], merged_sb[:, kc, :], kc == 0, kc == DC - 1, (sbs[0], merged_b[kc]), (po_b,))
                tt(DVE, h_sb[:, c, ssl(s)], po[:], h_sb[:, c, ssl(s)], ALU.add, (po_b, h_b[c][s]), (h_b[c][s],))

    def ple(l, t):
        g0 = t * T
        pg.phase = "ple_norm"
        rms_norm_tile(l, 3, xn_sb, xn_b)
        pg.phase = "ple"
        wload(waux_sb[:], w_pp[l].rearrange("(kc p) n -> p kc n", p=128), waux_b)
        dma(POOL, pbuf_sb[:], pT[l].rearrange("(kc p) s -> p kc s", p=128)[:, :, g0:g0 + T], (), (pbuf_b,))
        for c in range(DC):
            slot, sbs = wslot()
            sv = slot_view(slot, DC, 128)
            wload(sv, wview(w_pg, l, c * 128, 128), sbs[0])
            for s in range(NS):
                pgt, pgt_b = bank()
                ppp, ppp_b = bank()
                for kc in range(DC):
                    mm(pgt[:], sv[:, kc, :], xn_sb[:, kc, ssl(s)], kc == 0, kc == DC - 1, (sbs[0], xn_b[kc][s]), (pgt_b,))
                for kc in range(2):
                    mm(ppp[:], waux_sb[:, kc, c * 128:(c + 1) * 128], pbuf_sb[:, kc, ssl(s)], kc == 0, kc == 1,
                       (waux_b, pbuf_b), (ppp_b,))
                sg, sg_b = tmp_f()
                act(sg[:], pgt[:], AF.Sigmoid, (pgt_b,), (sg_b,))
                tt(DVE, sg[:], sg[:], ppp[:], ALU.mult, (sg_b, ppp_b), (sg_b,))
                tt(DVE, h_sb[:, c, ssl(s)], h_sb[:, c, ssl(s)], sg[:], ALU.add, (sg_b, h_b[c][s]), (h_b[c][s],))

    for l in range(depth):
        for t in range(NT):
            g0 = t * T
            src = xT if l == 0 else hT_d
            for c in range(DC):
                rd = () if l == 0 else (hT_b[c][t],)
                dma(SP, h_sb[:, c, :], src[c * 128:(c + 1) * 128, g0:g0 + T], rd, tuple(h_b[c]))
            def dump():
                for c in range(DC):
                    dma(SP, outT[c * 128:(c + 1) * 128, g0:g0 + T], h_sb[:, c, :], tuple(h_b[c]), ())
            stages = [lambda: ffn(l, 0), lambda: (mixer_pre(l, t), attention(l, t), merge_and_out(l, t)),
                      lambda: ffn(l, 1), lambda: ple(l, t)]
            if dbg is not None:
                for si in range(dbg):
                    stages[si]()
                dump()
                continue
            for st in stages:
                st()
            if l < depth - 1:
                for c in range(DC):
                    dma(SP, hT_d[c * 128:(c + 1) * 128, g0:g0 + T], h_sb[:, c, :], tuple(h_b[c]), (hT_b[c][t],))
            else:
                for s in range(NS):
                    pN, pN_b = bank()
                    for c in range(DC):
                        sq, sq_b = tmp_b()
                        act(sq[:], h_sb[:, c, ssl(s)], AF.Square, (h_b[c][s],), (sq_b,))
                        mm(pN[:], ones1k[:], sq[:], c == 0, c == DC - 1, (sq_b, const_b), (pN_b,))
                    sd, sd_b = tmp_f()
                    act(sd[:], pN[:], AF.Ln, (pN_b,), (sd_b,), scale=1.0, bias=EPS)
                    act(sd[:], sd[:], AF.Exp, (sd_b,), (sd_b,), scale=-0.5)
                    for c in range(DC):
                        oi = nxt("ost", 2)
                        ot, ot_b = ost_sb[oi], ost_b[oi]
                        stt(DVE, ot[:], h_sb[:, c, ssl(s)], col(o_final + c), sd[:], ALU.mult, ALU.mult,
                            (h_b[c][s], sd_b, small_b), (ot_b,))
                        dma(SP, outT[c * 128:(c + 1) * 128, g0 + s * SUB:g0 + (s + 1) * SUB], ot[:], (ot_b,), ())
    pg.finish()
    pg.emit()
    import os
    if os.environ.get("KDBG_PHASES"):
        import json
        with open(os.environ["KDBG_PHASES"], "w") as f:
            json.dump({e: [o.phase for o in pg.ops if o.eng == e and not o.dma] for e in Prog.ENGS}, f)
    return nc


def raise_if_none(x):
    if x is None:
        raise RuntimeError("AP.bitcast unavailable")


def pack_small(depth, ffn1_norm, mix_norm, ffn2_norm, ple_norm, final_norm, q_norm, kv_norm, conv_w):
    cols = []
    for l in range(depth):
        for g in (ffn1_norm, mix_norm, ffn2_norm, ple_norm):
            cols.append(np.asarray(g[l], np.float32).reshape(8, 128).T)
    cols.append(np.asarray(final_norm, np.float32).reshape(8, 128).T)
    for l in range(depth):
        cols.append(np.asarray(q_norm[l], np.float32).reshape(3, 128).T)
    for l in range(depth):
        cols.append(np.asarray(kv_norm[l], np.float32).reshape(2, 128).T)
    for l in range(depth):
        cw = np.asarray(conv_w[l], np.float32).reshape(3, 4, 128)
        cols.append(cw.transpose(2, 0, 1).reshape(128, 12))
    invf = (10000.0 ** (-np.arange(0, ROPE, 2, dtype=np.float32) / ROPE)).astype(np.float32)
    iv = np.zeros((128, 1), np.float32)
    iv[0:32, 0] = invf
    iv[32:64, 0] = invf
    cols.append(iv)
    return np.ascontiguousarray(np.concatenate(cols, axis=1), dtype=np.float32)


_NC_CACHE = {}


def run(inputs, S, depth, ncores, T=1024, trace=False, dbg=None):
    key = (S, depth, T, dbg)
    if key not in _NC_CACHE:
        _NC_CACHE[key] = build_program(S, depth, T, dbg=dbg)
    nc = _NC_CACHE[key]
    f = lambda k: np.ascontiguousarray(np.asarray(inputs[k], dtype=np.float32))
    small = pack_small(depth, inputs["ffn1_norm"], inputs["mix_norm"], inputs["ffn2_norm"], inputs["ple_norm"],
                       inputs["final_norm"], inputs["q_norm"], inputs["kv_norm"], inputs["conv_w"])
    shared = {k: f(k) for k in ("ffn1_w_gu", "ffn2_w_gu", "ffn1_w_down", "ffn2_w_down", "w_in", "w_conv_out",
                                "w_uq", "w_ukv", "w_mla_out", "w_o", "w_ple_gate", "w_ple_proj")}
    shared["small"] = small
    x = np.asarray(inputs["x"], np.float32)
    p = np.asarray(inputs["p"], np.float32)
    pos = np.asarray(inputs["positions"]).astype(np.int32)
    in_maps = []
    for b in range(ncores):
        m = dict(shared)
        m["xT"] = np.ascontiguousarray(x[b].T)
        m["pT"] = np.ascontiguousarray(p[:, b].transpose(0, 2, 1))
        m["pos64"] = np.ascontiguousarray(np.broadcast_to(pos[b][None, :], (64, S)))
        in_maps.append(m)
    res = run_bass_kernel_spmd(nc, in_maps, core_ids=list(range(ncores)), trace=trace)
    out = np.stack([np.ascontiguousarray(r["outT"].T) for r in res.results], axis=0)
    return out.astype(np.float32), res


def kernel(**inputs):
    x = inputs["x"]
    B, S, _ = x.shape
    depth = inputs["ffn1_w_gu"].shape[0]
    out, _ = run(inputs, S, depth, B)
    return out
```
